# Optimizing a Trainium2 kernel written in Bass

```python
import jax, jax.numpy as jnp
from jax import lax
import numpy as np

D_MODEL = 4096
BATCH = 1
SEQ = 16384
DEPTH = 1
DEC_BATCH = 2
DEC_SEQ = 4096
PAST_LEN = 128

GRID_W = 64
N_MEM = 256
NA_HEADS = 16
NA_HEAD_DIM = 128
NA_WIDTH = NA_HEADS * NA_HEAD_DIM
NA_MAX_KH = 8
NA_KW = 16
POOL_WINDOWS = (2, 4, 8, 16)
POOL_GROUPS = 4
POOL_GROUP_DIM = 256
POOL_WIDTH = POOL_GROUPS * POOL_GROUP_DIM
MEM_HEADS = 4
MEM_HEAD_DIM = 256
MEM_WIDTH = MEM_HEADS * MEM_HEAD_DIM
N_BRANCH = 3
IN_WIDTH = 3 * NA_WIDTH + POOL_WIDTH + MEM_WIDTH + N_BRANCH * D_MODEL
D_FF = 11008
RMS_EPS = 1e-6

kernel_name = "hybrid_natten_pool_memxattn_macaron_encoder"


def _rmsnorm(x, g):
    xf = x.astype(jnp.float32)
    y = xf * lax.rsqrt(jnp.mean(xf * xf, axis=-1, keepdims=True) + RMS_EPS)
    return (y * g.astype(jnp.float32)).astype(x.dtype)


def _swiglu(h, w_gate, w_up, w_down):
    return (jax.nn.silu(h @ w_gate) * (h @ w_up)) @ w_down


def _neighbourhood_attention(q, k, v, rpb):
    b, L, h, dh = q.shape
    rows = L // GRID_W
    kh = min(NA_MAX_KH, rows)
    n_keys = kh * NA_KW
    cols = jnp.arange(GRID_W)
    col_start = jnp.clip(cols - NA_KW // 2, 0, GRID_W - NA_KW)
    col_idx = col_start[:, None] + jnp.arange(NA_KW)[None, :]
    col_off = col_idx - cols[:, None] + (NA_KW - 1)
    scale = dh ** -0.5
    q_rows = q.reshape(b, rows, GRID_W, h, dh)

    def one_row(r):
        row_start = jnp.clip(r - kh // 2, 0, rows - kh)
        row_ids = row_start + jnp.arange(kh)
        key_idx = (row_ids[None, :, None] * GRID_W + col_idx[:, None, :]).reshape(GRID_W, n_keys)
        row_off = row_ids - r + (NA_MAX_KH - 1)
        bias = rpb[:, row_off[None, :, None], col_off[:, None, :]].reshape(h, GRID_W, n_keys)
        k_g = k[:, key_idx]
        v_g = v[:, key_idx]
        q_r = lax.dynamic_index_in_dim(q_rows, r, axis=1, keepdims=False)
        s = jnp.einsum('bchd,bckhd->bhck', q_r, k_g).astype(jnp.float32) * scale
        s = s + bias.astype(jnp.float32)[None]
        p = jax.nn.softmax(s, axis=-1).astype(v.dtype)
        return jnp.einsum('bhck,bckhd->bchd', p, v_g)

    o = lax.map(one_row, jnp.arange(rows))
    return jnp.moveaxis(o, 0, 1).reshape(b, L, h * dh)


def _pool_mixer(u, w_pool, pool_scale):
    b, L, _ = u.shape
    ug = u.astype(jnp.float32).reshape(b, L, POOL_GROUPS, POOL_GROUP_DIM)
    csum = jnp.concatenate(
        [jnp.zeros((b, 1, POOL_GROUPS, POOL_GROUP_DIM), jnp.float32), jnp.cumsum(ug, axis=1)], axis=1)
    t = jnp.arange(L)
    outs = []
    for g, w in enumerate(POOL_WINDOWS):
        lo = jnp.clip(t - w // 2, 0, L)
        hi = jnp.clip(t + w // 2, 0, L)
        window_sum = csum[:, hi, g] - csum[:, lo, g]
        mean = window_sum / (hi - lo).astype(jnp.float32)[None, :, None]
        outs.append(mean - ug[:, :, g])
    pooled = jnp.stack(outs, axis=2).astype(u.dtype)
    mixed = jnp.einsum('blgc,gcd->blgd', pooled, w_pool).reshape(b, L, POOL_WIDTH)
    return mixed * pool_scale


def _memory_xattn(q_c, mem, g_mem, w_mem_kv):
    b, L, _ = q_c.shape
    kv = (_rmsnorm(mem, g_mem) @ w_mem_kv).reshape(b, mem.shape[1], 2, MEM_HEADS, MEM_HEAD_DIM)
    k, v = kv[:, :, 0], kv[:, :, 1]
    q = q_c.reshape(b, L, MEM_HEADS, MEM_HEAD_DIM)
    s = jnp.einsum('blhd,bmhd->bhlm', q, k).astype(jnp.float32) * (MEM_HEAD_DIM ** -0.5)
    p = jax.nn.softmax(s, axis=-1).astype(v.dtype)
    return jnp.einsum('bhlm,bmhd->blhd', p, v).reshape(b, L, MEM_WIDTH)


def _layer(x, mem,
           g_ffn1_pre, w1_gate, w1_up, w1_down, g_ffn1_post,
           g_mix_pre, w_in, rpb, w_pool, pool_scale, g_mem, w_mem_kv,
           w_a_out, w_b_out, w_c_out, b_gate, w_o, g_mix_post,
           g_ffn2_pre, w2_gate, w2_up, w2_down, g_ffn2_post, g_final):
    b, L, d = x.shape
    x = x + 0.5 * _rmsnorm(_swiglu(_rmsnorm(x, g_ffn1_pre), w1_gate, w1_up, w1_down), g_ffn1_post)
    h = _rmsnorm(x, g_mix_pre)
    z = h @ w_in
    o1 = NA_WIDTH
    o2 = 2 * NA_WIDTH
    o3 = 3 * NA_WIDTH
    o4 = o3 + POOL_WIDTH
    o5 = o4 + MEM_WIDTH
    q_a, k_a, v_a, u_b, q_c, gate_logits = jnp.split(z, [o1, o2, o3, o4, o5], axis=-1)
    hs = (b, L, NA_HEADS, NA_HEAD_DIM)
    y_a = _neighbourhood_attention(q_a.reshape(hs), k_a.reshape(hs), v_a.reshape(hs), rpb) @ w_a_out
    y_b = _pool_mixer(u_b, w_pool, pool_scale) @ w_b_out
    y_c = _memory_xattn(q_c, mem, g_mem, w_mem_kv) @ w_c_out
    gates = jax.nn.sigmoid(gate_logits.reshape(b, L, N_BRANCH, d) + b_gate)
    merged = gates[:, :, 0] * y_a + gates[:, :, 1] * y_b + gates[:, :, 2] * y_c
    x = x + _rmsnorm(merged @ w_o, g_mix_post)
    x = x + 0.5 * _rmsnorm(_swiglu(_rmsnorm(x, g_ffn2_pre), w2_gate, w2_up, w2_down), g_ffn2_post)
    return _rmsnorm(x, g_final)


def setup_inputs(seed: int = 0) -> dict:
    key = jax.random.key(seed)
    ks = iter(jax.random.split(key, 40))
    f32 = jnp.float32

    def nrm(shape, scale):
        return jax.random.normal(next(ks), shape, f32) * scale

    def gain(shape):
        return 1.0 + 0.02 * jax.random.normal(next(ks), shape, f32)

    D = D_MODEL
    L_ = DEPTH
    return {
        "x_prompt": nrm((BATCH, SEQ, D), 1.0),
        "x_sample": nrm((DEC_BATCH, DEC_SEQ, D), 1.0),
        "mem_prompt": nrm((BATCH, N_MEM, D), 1.0),
        "mem_sample": nrm((DEC_BATCH, N_MEM, D), 1.0),
        "g_ffn1_pre": gain((L_, D)),
        "w1_gate": nrm((L_, D, D_FF), D ** -0.5),
        "w1_up": nrm((L_, D, D_FF), D ** -0.5),
        "w1_down": nrm((L_, D_FF, D), D_FF ** -0.5),
        "g_ffn1_post": gain((L_, D)),
        "g_mix_pre": gain((L_, D)),
        "w_in": nrm((L_, D, IN_WIDTH), D ** -0.5),
        "rpb": nrm((L_, NA_HEADS, 2 * NA_MAX_KH - 1, 2 * NA_KW - 1), 0.1),
        "w_pool": nrm((L_, POOL_GROUPS, POOL_GROUP_DIM, POOL_GROUP_DIM), POOL_GROUP_DIM ** -0.5),
        "pool_scale": gain((L_, POOL_WIDTH)),
        "g_mem": gain((L_, D)),
        "w_mem_kv": nrm((L_, D, 2 * MEM_WIDTH), D ** -0.5),
        "w_a_out": nrm((L_, NA_WIDTH, D), NA_WIDTH ** -0.5),
        "w_b_out": nrm((L_, POOL_WIDTH, D), POOL_WIDTH ** -0.5),
        "w_c_out": nrm((L_, MEM_WIDTH, D), MEM_WIDTH ** -0.5),
        "b_gate": nrm((L_, N_BRANCH, D), 0.01),
        "w_o": nrm((L_, D, D), D ** -0.5),
        "g_mix_post": gain((L_, D)),
        "g_ffn2_pre": gain((L_, D)),
        "w2_gate": nrm((L_, D, D_FF), D ** -0.5),
        "w2_up": nrm((L_, D, D_FF), D ** -0.5),
        "w2_down": nrm((L_, D_FF, D), D_FF ** -0.5),
        "g_ffn2_post": gain((L_, D)),
        "g_final": gain((L_, D)),
    }


def reference(x_prompt, x_sample, mem_prompt, mem_sample,
              g_ffn1_pre, w1_gate, w1_up, w1_down, g_ffn1_post,
              g_mix_pre, w_in, rpb, w_pool, pool_scale, g_mem, w_mem_kv,
              w_a_out, w_b_out, w_c_out, b_gate, w_o, g_mix_post,
              g_ffn2_pre, w2_gate, w2_up, w2_down, g_ffn2_post, g_final):
    y_prompt = x_prompt
    y_sample = x_sample
    for i in range(DEPTH):
        p = [a[i] for a in (g_ffn1_pre, w1_gate, w1_up, w1_down, g_ffn1_post,
                            g_mix_pre, w_in, rpb, w_pool, pool_scale, g_mem, w_mem_kv,
                            w_a_out, w_b_out, w_c_out, b_gate, w_o, g_mix_post,
                            g_ffn2_pre, w2_gate, w2_up, w2_down, g_ffn2_post, g_final)]
        y_prompt = _layer(y_prompt, mem_prompt, *p)
        y_sample = _layer(y_sample, mem_sample, *p)
    return (y_prompt, y_sample)
```

```python
import numpy as np
import concourse.bass as bass
import concourse.mybir as mybir
from concourse.bass_utils import run_bass_kernel_spmd

F32 = mybir.dt.float32
BF16 = mybir.dt.bfloat16
AF = mybir.ActivationFunctionType
ALU = mybir.AluOpType
NEG = -30000.0
EPS = 1e-6
SAME_ENGINE_SYNC = True


class Cfg:
    def __init__(self, DC=32, FC=86, NH=16, MH=4, seq_tiles=(32, 8, 8), n_cores=8):
        self.DC, self.FC, self.NH, self.MH = DC, FC, NH, MH
        self.D, self.DFF = 128 * DC, 128 * FC
        self.NAW, self.PW, self.MW = NH * 128, 1024, MH * 256
        self.INW = 3 * self.NAW + self.PW + self.MW + 3 * self.D
        self.seq_tiles = tuple(seq_tiles)
        self.n_cores = n_cores
        tot = sum(seq_tiles)
        assert tot % n_cores == 0
        self.own = tot // n_cores
        self.next = self.own + 1
        self.T = 512
        self.NMEM = 256
        self.oq, self.ok, self.ov = 0, self.NAW, 2 * self.NAW
        self.ou = 3 * self.NAW
        self.oqc = self.ou + self.PW
        self.og = self.oqc + self.MW


class Sem:
    def __init__(self, handle, name):
        self.h = handle
        self.name = name
        self.count = 0


class Buf:
    __slots__ = ("w", "r", "name")

    def __init__(self, name=""):
        self.w = None
        self.r = []
        self.name = name


class Engine:
    def __init__(self, name, sem):
        self.name = name
        self.sem = sem
        self.ops = []
        self.seen = {}


class Prog:
    def __init__(self, nc):
        self.nc = nc
        self.E = {}
        self.nsem = 0

    def new_sem(self, name):
        self.nsem += 1
        return Sem(self.nc.alloc_semaphore(name), name)

    def add_engine(self, name):
        self.E[name] = Engine(name, self.new_sem("e_" + name))

    def _waits(self, eng, reads, writes, skip_self):
        need = {}
        for b in reads:
            if b.w is not None:
                s, v = b.w
                if need.get(s, 0) < v:
                    need[s] = v
        for b in writes:
            if b.w is not None:
                s, v = b.w
                if need.get(s, 0) < v:
                    need[s] = v
            for (s, v) in b.r:
                if need.get(s, 0) < v:
                    need[s] = v
        waits = []
        for s, v in need.items():
            if s is eng.sem and skip_self:
                continue
            if eng.seen.get(s, 0) < v:
                eng.seen[s] = v
                waits.append((s, v))
        return waits

    def op(self, ename, fn, reads=(), writes=(), inc=True):
        eng = self.E[ename]
        skip_self = (ename == "pe") or (not SAME_ENGINE_SYNC)
        waits = self._waits(eng, reads, writes, skip_self)
        if inc:
            eng.sem.count += 1
            stamp = (eng.sem, eng.sem.count)
        else:
            stamp = (eng.sem, eng.sem.count + 1)
        for b in writes:
            b.w = stamp
            b.r = []
        for b in reads:
            if not b.r or b.r[-1] != stamp:
                b.r.append(stamp)
        eng.ops.append((waits, fn, eng.sem if inc else None, 1))

    def dma(self, qname, out, in_, reads, writes, sem):
        eng = self.E[qname]
        waits = self._waits(eng, reads, writes, False)
        if sem.count > 0 and eng.seen.get(sem, 0) < sem.count:
            eng.seen[sem] = sem.count
            waits.append((sem, sem.count))
        sem.count += 16
        stamp = (sem, sem.count)
        for b in writes:
            b.w = stamp
            b.r = []
        for b in reads:
            b.r.append(stamp)

        def fn(E, out=out, in_=in_):
            return E.dma_start(out=out, in_=in_)

        eng.ops.append((waits, fn, sem, 16))

    def final_wait(self, qname, sems):
        eng = self.E[qname]
        waits = [(s, s.count) for s in sems if s.count > 0]
        eng.ops.append((waits, None, None, 0))

    def replay(self, ename):
        ops = self.E[ename].ops

        def run(E):
            for (waits, fn, sem, amt) in ops:
                for (s, v) in waits:
                    E.wait_ge(s.h, v)
                if fn is None:
                    continue
                ins = fn(E)
                if sem is not None:
                    ins.then_inc(sem.h, amt)

        return run


class Reg:
    def __init__(self, t, b=None, sem=None):
        self.t = t
        self.b = b if b is not None else Buf()
        self.sem = sem


def build_program(cfg):
    nc = bass.Bass("TRN2", target_bir_lowering=False)
    P = Prog(nc)
    for n in ("pe", "act", "dve", "pool", "sp"):
        P.add_engine(n)
    DC, FC, NH, MH, D, DFF, T = cfg.DC, cfg.FC, cfg.NH, cfg.MH, cfg.D, cfg.DFF, cfg.T
    own, NEXT = cfg.own, cfg.next
    NTOK = NEXT * T

    def din(name, shape, dt=F32):
        return nc.dram_tensor(name, list(shape), dt, kind="ExternalInput").ap()

    def dscr(name, shape, dt):
        return nc.dram_tensor(name, list(shape), dt).ap()

    xT = din("xT", [D, NTOK])
    memT = din("memT", [own, D, 256])
    w1g, w1u, w1d = din("w1_gate", [D, DFF]), din("w1_up", [D, DFF]), din("w1_down", [DFF, D])
    w2g, w2u, w2d = din("w2_gate", [D, DFF]), din("w2_up", [D, DFF]), din("w2_down", [DFF, D])
    w_in = din("w_in", [D, cfg.INW])
    w_kv = din("w_mem_kv", [D, 2 * cfg.MW])
    w_ao, w_bo, w_co = din("w_a_out", [cfg.NAW, D]), din("w_b_out", [cfg.PW, D]), din("w_c_out", [cfg.MW, D])
    w_o = din("w_o", [D, D])
    w_pool = din("w_pool", [4 * 256, 256])
    NG = 8
    gains_d = din("gains", [128, NG * DC])
    bgate_d = din("bgate", [128, 3 * DC])
    pscale_d = din("pscale", [128, 8])
    biasT_d = din("biasT", [NH, 128, 22 * 64])
    maskL_d = din("maskL", [own, 8, 1024])
    pvalid_d = din("pvalid", [own, 128, 528])
    pinv_d = din("pinv", [own, 128, 4 * 512])
    sel_d = din("sel", [8, 512])
    ident_d = din("ident", [128, 128])
    yT = nc.dram_tensor("yT", [D, own * T], F32, kind="ExternalOutput").ap()

    x1T = dscr("x1T", [D, NTOK], F32)
    x2T = dscr("x2T", [D, own * T], F32)
    qTs = dscr("qTs", [cfg.NAW, NTOK], BF16)
    kTs = dscr("kTs", [cfg.NAW, NTOK], BF16)
    vS = dscr("vS", [NTOK, cfg.NAW], BF16)
    uTs = dscr("uTs", [cfg.PW, NTOK], F32)
    qcTs = dscr("qcTs", [cfg.MW, NTOK], BF16)
    b_x1 = [Buf() for _ in range(NEXT)]
    b_q = [Buf() for _ in range(NEXT)]
    b_k = [Buf() for _ in range(NEXT)]
    b_v = [Buf() for _ in range(NEXT)]
    b_u = [Buf() for _ in range(NEXT)]
    b_qc = [Buf() for _ in range(NEXT)]
    b_x2 = [Buf() for _ in range(own)]
    b_y = [Buf() for _ in range(own)]

    ARENA = 210944
    arena = nc.alloc_sbuf_tensor("arena", [128, ARENA // 4], F32)
    base = nc.lookup_mloc(arena).addr

    def at(name, shape, dt, off):
        return nc.alloc_sbuf_tensor_at(name, list(shape), dt, offset=off)

    R_ACT = base
    R_Y = R_ACT + 86 * 1024
    R_W = R_Y + 65536
    NWS = 4
    NXS = 6
    R_M = R_W + NWS * 8192

    act_b = [Buf("act%d" % i) for i in range(86)]
    y_b = [Buf("y%d" % i) for i in range(32)]
    aT = at("aT", [128, 86, 512], BF16, R_ACT)
    Yf = at("Yf", [128, 32, 512], F32, R_Y)
    Hb = at("Hb", [128, 32, 512], BF16, R_Y)
    H2 = at("H2", [128, 32, 512], BF16, R_ACT)

    def hb_bufs(c):
        return [y_b[c // 2]]

    wslots = []
    for i in range(NWS):
        t = at("ws%d" % i, [128, 32, 128], BF16, R_W + i * 8192)
        r = Reg(t, Buf("ws%d" % i), P.new_sem("ws%d" % i))
        r.tv = at("wsv%d" % i, [128, 8, 512], BF16, R_W + i * 8192)
        r.sem2 = P.new_sem("wst%d" % i)
        wslots.append(r)
    wctr = [0]

    def next_wslot():
        s = wslots[wctr[0] % NWS]
        wctr[0] += 1
        return s

    off = R_M
    XS = []
    XS_BASE = off
    for i in range(NXS):
        XS.append(Reg(at("xs%d" % i, [128, 512], F32, off), Buf(), P.new_sem("xs%d" % i)))
        off = (off + 2048 + 31) // 32 * 32
    xsctr = [0]
    FT = []
    for i in range(3):
        FT.append(Reg(at("ft%d" % i, [128, 512], F32, off), Buf()))
        off = (off + 2048 + 31) // 32 * 32
    ftctr = [0]

    def next_ft():
        s = FT[ftctr[0] % 3]
        ftctr[0] += 1
        return s

    RSTD = Reg(at("rstd", [128, 512], F32, off)); off = (off + 2048 + 31) // 32 * 32
    ones_f = Reg(at("ones_f", [128, 128], F32, off)); off = (off + 512 + 31) // 32 * 32
    ones_b = Reg(at("ones_b", [128, 128], BF16, off)); off = (off + 256 + 31) // 32 * 32
    ident_b = Reg(at("ident_b", [128, 128], BF16, off), sem=P.new_sem("identl")); off = (off + 256 + 31) // 32 * 32
    SEL = Reg(at("sel", [8, 512], BF16, off), sem=P.new_sem("sell")); off = (off + 1024 + 31) // 32 * 32
    gains = Reg(at("gains", [128, NG * DC], F32, off), sem=P.new_sem("gl")); off = (off + NG * DC * 4 + 31) // 32 * 32
    hgain = Reg(at("hgain", [128, 2 * DC], F32, off)); off = (off + 2 * DC * 4 + 31) // 32 * 32
    bgate = Reg(at("bgate", [128, 3 * DC], F32, off), sem=P.new_sem("bgl")); off = (off + 3 * DC * 4 + 31) // 32 * 32
    pscale = Reg(at("pscale", [128, 8], F32, off), sem=P.new_sem("psl")); off = (off + 32 + 31) // 32 * 32
    epsb = Reg(at("epsb", [128, 8], F32, off)); off = (off + 32 + 31) // 32 * 32
    assert off <= base + ARENA, (off - base, ARENA)
    G_F1PRE, G_F1POST, G_MIXPRE, G_MEM, G_MIXPOST, G_F2PRE, G_F2POST, G_FINAL = range(8)

    def gcol(gi, c):
        return gains.t[:, gi * DC + c: gi * DC + c + 1]

    PS = [Reg(nc.alloc_psum_tensor("ps%d" % i, [128, 512], F32), Buf("ps%d" % i)) for i in range(8)]

    cdiv = lambda a, b: (a + b - 1) // b
    nA = 2 * FC + DC * cdiv(FC, 32) + (2 * NH + 8 + 2 * MH) * cdiv(DC, 32) + (cfg.NAW // 512) * max(1, DC // 8)
    nB = 2 * MH * cdiv(DC, 32) + (cfg.MW // 512) * max(1, DC // 8) + 6 * DC + DC * cdiv(DC, 32) + 2 * FC + DC * cdiv(FC, 32)
    WCH = 200
    wscr = {ph: [dscr("wscr%s%d" % (ph, i), [min(WCH, n - i * WCH), 128, 4096], BF16) for i in range(cdiv(n, WCH))]
            for ph, n in (("A", nA), ("B", nB))}
    wscr_b = {"A": [Buf() for _ in range(nA)], "B": [Buf() for _ in range(nB)]}
    wctx = {"ph": "A", "pass0": True, "key": 0, "cc": 0}
    keyF2 = 2 * MH * cdiv(DC, 32) + (cfg.MW // 512) * max(1, DC // 8) + 6 * DC + DC * cdiv(DC, 32)
    conv_list = []
    preconv = set()
    CONV_EVERY = 5

    def maybe_convert():
        if wctx["ph"] != "A" or wctx["pass0"] or not conv_list:
            return
        wctx["cc"] += 1
        if wctx["cc"] % CONV_EVERY:
            return
        key, src_ap, nk = conv_list.pop(0)
        s = next_wslot()
        dst = s.t[:, 0:nk, :]
        img = wscr["B"][key // WCH][key % WCH, :, 0:nk * 128].rearrange("p (c f) -> p c f", f=128)
        P.dma("pool", dst, src_ap.rearrange("(c p) f -> p c f", p=128), reads=[], writes=[s.b], sem=s.sem)
        P.dma("sp", img, dst, reads=[s.b], writes=[wscr_b["B"][key]], sem=s.sem2)
        preconv.add(key)

    def load_w(src_ap, nk, wide=False):
        s = next_wslot()
        width = 512 if wide else 128
        dst = s.tv[:, 0:nk, :] if wide else s.t[:, 0:nk, :]
        ph, key = wctx["ph"], wctx["key"]
        wctx["key"] += 1
        img = wscr[ph][key // WCH][key % WCH, :, 0:nk * width].rearrange("p (c f) -> p c f", f=width)
        if wctx["pass0"] and not (ph == "B" and key in preconv):
            P.dma("pool", dst, src_ap.rearrange("(c p) f -> p c f", p=128), reads=[], writes=[s.b], sem=s.sem)
            P.dma("sp", img, dst, reads=[s.b], writes=[wscr_b[ph][key]], sem=s.sem2)
        else:
            P.dma("pool", dst, img, reads=[wscr_b[ph][key]], writes=[s.b], sem=s.sem)
        return s

    P.op("dve", lambda E: E.memset(ones_f.t[:], 1.0), writes=[ones_f.b])
    P.op("dve", lambda E: E.memset(ones_b.t[:], 1.0), writes=[ones_b.b])
    P.op("dve", lambda E: E.memset(epsb.t[:], EPS), writes=[epsb.b])
    P.dma("pool", ident_b.t[:], ident_d[:, :], [], [ident_b.b], ident_b.sem)
    P.dma("pool", SEL.t[:], sel_d[:, :], [], [SEL.b], SEL.sem)
    P.dma("sp", gains.t[:], gains_d[:, :], [], [gains.b], gains.sem)
    P.dma("sp", bgate.t[:], bgate_d[:, :], [], [bgate.b], bgate.sem)
    P.dma("sp", pscale.t[:], pscale_d[:, :], [], [pscale.b], pscale.sem)
    P.op("dve", lambda E: E.tensor_scalar(hgain.t[:, 0:DC], gains.t[:, G_F1POST * DC:(G_F1POST + 1) * DC], 0.5, None, ALU.mult),
         reads=[gains.b], writes=[hgain.b])
    P.op("dve", lambda E: E.tensor_scalar(hgain.t[:, DC:2 * DC], gains.t[:, G_F2POST * DC:(G_F2POST + 1) * DC], 0.5, None, ALU.mult),
         reads=[gains.b], writes=[hgain.b])

    STAT = PS[6]

    def stats_sq(src_ap, src_bufs, n=512):
        ft = next_ft()
        P.op("dve", lambda E: E.tensor_tensor(ft.t[:, 0:n], src_ap, src_ap, ALU.mult), reads=src_bufs, writes=[ft.b])
        return ft

    def stats_mm(ft, c, nchunks, n=512):
        P.op("pe", lambda E: E.matmul(STAT.t[:, 0:n], lhsT=ones_f.t[:], rhs=ft.t[:, 0:n], start=(c == 0), stop=(c == nchunks - 1)),
             reads=[ones_f.b, ft.b], writes=[STAT.b], inc=True)

    def stats_accum(src_ap, src_bufs, c, nchunks, n=512):
        stats_mm(stats_sq(src_ap, src_bufs, n), c, nchunks, n)

    pend = []

    def flush_pend():
        while pend:
            pend.pop(0)()

    def stats_finish(n=512):
        P.op("act", lambda E: E.activation(RSTD.t[:, 0:n], STAT.t[:, 0:n], AF.Sqrt, bias=epsb.t[:, 0:1], scale=1.0 / D),
             reads=[STAT.b, epsb.b], writes=[RSTD.b])
        P.op("dve", lambda E: E.reciprocal(RSTD.t[:, 0:n], RSTD.t[:, 0:n]),
             reads=[RSTD.b], writes=[RSTD.b])

    def xs_load(src_ap, src_bufs, n=512):
        s = XS[xsctr[0] % NXS]
        xsctr[0] += 1
        P.dma("sp", s.t[:, 0:n], src_ap, reads=src_bufs, writes=[s.b], sem=s.sem)
        return s

    def prenorm_from_dram(src, src_bufs, t0, gi, dst, dst_bufs_fn, n=512, nch=None):
        nch = DC if nch is None else nch
        for c in range(nch):
            s = xs_load(src[c * 128:(c + 1) * 128, t0:t0 + n], src_bufs, n)
            stats_accum(s.t[:, 0:n], [s.b], c, nch, n)
        stats_finish(n)
        for c in range(nch):
            s = xs_load(src[c * 128:(c + 1) * 128, t0:t0 + n], src_bufs, n)
            P.op("dve", (lambda c, s: lambda E: E.scalar_tensor_tensor(dst[:, c, 0:n], s.t[:, 0:n], gcol(gi, c), RSTD.t[:, 0:n],
                                                                      ALU.mult, ALU.mult))(c, s),
                 reads=[s.b, RSTD.b, gains.b], writes=dst_bufs_fn(c))

    def prenorm_from_Y(gi, dst, dst_bufs_fn, stats_done=False):
        for c in range(DC):
            if not stats_done:
                stats_accum(Yf[:, c, :], [y_b[c]], c, DC)
        stats_finish()
        for c in range(DC):
            P.op("dve", (lambda c: lambda E: E.scalar_tensor_tensor(dst[:, c, :], Yf[:, c, :], gcol(gi, c), RSTD.t[:],
                                                                   ALU.mult, ALU.mult))(c),
                 reads=[y_b[c], RSTD.b, gains.b], writes=dst_bufs_fn(c))

    def proj_rows(wsrc_fn, kchunks, rhs_fn, rhs_bufs_fn, nout, evac_fn, banks):
        for o in range(nout):
            ps = PS[banks[o % len(banks)]]
            k0 = 0
            first = True
            while k0 < kchunks:
                nk = min(32, kchunks - k0)
                s = load_w(wsrc_fn(o, k0, nk), nk)
                for kk in range(nk):
                    k = k0 + kk
                    last = (k == kchunks - 1)
                    rhs = rhs_fn(k)
                    P.op("pe", (lambda ps, s, kk, rhs, first, last: lambda E: E.matmul(
                        ps.t[:, 0:rhs.shape[-1]], lhsT=s.t[:, kk, :], rhs=rhs, start=first, stop=last))(ps, s, kk, rhs, first, last),
                         reads=[s.b] + rhs_bufs_fn(k), writes=[ps.b], inc=last)
                    first = False
                k0 += nk
            flush_pend()
            evac_fn(o, ps)
        flush_pend()

    def residual_finish(src, src_bufs, t0, gain_ap_fn, accum_stats=False):
        stats_finish()
        for c in range(DC):
            s = xs_load(src[c * 128:(c + 1) * 128, t0:t0 + 512], src_bufs)
            P.op("dve", (lambda c: lambda E: E.scalar_tensor_tensor(Yf[:, c, :], Yf[:, c, :], gain_ap_fn(c), RSTD.t[:],
                                                                   ALU.mult, ALU.mult))(c),
                 reads=[y_b[c], RSTD.b, gains.b, hgain.b], writes=[y_b[c]])
            P.op("dve", (lambda c, s: lambda E: E.tensor_tensor(Yf[:, c, :], Yf[:, c, :], s.t[:], ALU.add))(c, s),
                 reads=[y_b[c], s.b], writes=[y_b[c]])
            if accum_stats:
                stats_accum(Yf[:, c, :], [y_b[c]], c, DC)

    def evac_to_Y_with_stats(o, ps):
        P.op("act", lambda E: E.copy(Yf[:, o, :], ps.t[:]), reads=[ps.b], writes=[y_b[o]])
        ft = stats_sq(Yf[:, o, :], [y_b[o]])
        pend.append(lambda: stats_mm(ft, o, DC))

    sem_stY = P.new_sem("stY")

    def store_Y(dst, dst_bufs, t0):
        P.dma("sp", dst[:, t0:t0 + 512].rearrange("(c p) t -> p c t", p=128), Yf[:, 0:DC, :],
              reads=[y_b[c] for c in range(DC)], writes=dst_bufs, sem=sem_stY)

    def ffn_stage(src, src_bufs, t0, wg, wu, wd, g_pre, hg_off):
        prenorm_from_dram(src, src_bufs, t0, g_pre, Hb, hb_bufs)
        for f in range(FC):
            sg = load_w(wg[:, f * 128:(f + 1) * 128], DC)
            maybe_convert()
            su = load_w(wu[:, f * 128:(f + 1) * 128], DC)
            maybe_convert()
            pg, pu = PS[f % 2], PS[2 + f % 2]
            for (s, ps) in ((sg, pg), (su, pu)):
                for c in range(DC):
                    P.op("pe", (lambda ps, s, c: lambda E: E.matmul(ps.t[:], lhsT=s.t[:, c, :], rhs=Hb[:, c, :],
                                                                   start=(c == 0), stop=(c == DC - 1)))(ps, s, c),
                         reads=[s.b] + hb_bufs(c), writes=[ps.b], inc=(c == DC - 1))
            ft = next_ft()
            P.op("act", (lambda ft, pg: lambda E: E.activation(ft.t[:], pg.t[:], AF.Silu))(ft, pg), reads=[pg.b], writes=[ft.b])
            P.op("dve", (lambda ft, pu, f: lambda E: E.tensor_tensor(aT[:, f, :], ft.t[:], pu.t[:], ALU.mult))(ft, pu, f),
                 reads=[ft.b, pu.b], writes=[act_b[f]])
        proj_rows(lambda o, k0, nk: wd[k0 * 128:(k0 + nk) * 128, o * 128:(o + 1) * 128], FC,
                  lambda k: aT[:, k, :], lambda k: [act_b[k]], DC, evac_to_Y_with_stats, banks=(4, 5))
        residual_finish(src, src_bufs, t0, lambda c: hgain.t[:, hg_off * DC + c: hg_off * DC + c + 1], accum_stats=True)

    q_st = Reg(at("q_st", [128, 16, 512], BF16, R_ACT + 32768), None, P.new_sem("qst"))
    k_st = Reg(at("k_st", [128, 16, 512], BF16, R_ACT + 49152), None, P.new_sem("kst"))
    u_st = Reg(at("u_st", [128, 8, 512], F32, R_ACT + 65536), None, P.new_sem("ust"))
    qc_st = Reg(at("qc_st", [128, 6, 512], BF16, R_ACT + 81920), None, P.new_sem("qcst"))

    def h2_bufs(c):
        return [act_b[c]]

    def phaseA(i):
        t0 = i * T
        wctx.update(ph="A", pass0=(i == 0), key=0)
        ffn_stage(xT, [], t0, w1g, w1u, w1d, G_F1PRE, 0)
        store_Y(x1T, [b_x1[i]], t0)
        prenorm_from_Y(G_MIXPRE, H2, h2_bufs, stats_done=True)

        def mk_evac(stage, bufidx0, scale, width):
            def ev(o, ps):
                bl = [act_b[bufidx0 + (o * width) // 1 + j] for j in range(width)]
                if scale is None:
                    P.op("dve", lambda E: E.tensor_copy(stage[:, o, :], ps.t[:]), reads=[ps.b], writes=bl)
                else:
                    P.op("dve", lambda E: E.tensor_scalar(stage[:, o, :], ps.t[:], scale, None, ALU.mult), reads=[ps.b], writes=bl)
            return ev

        rhs_fn = lambda k: H2[:, k, :]
        rb_fn = lambda k: [act_b[k]]
        proj_rows(lambda o, k0, nk: w_in[k0 * 128:(k0 + nk) * 128, cfg.oq + o * 128: cfg.oq + (o + 1) * 128], DC,
                  rhs_fn, rb_fn, NH, mk_evac(q_st.t, 32, 128.0 ** -0.5, 1), banks=(0, 1, 2, 3))
        P.dma("sp", qTs[:, t0:t0 + T].rearrange("(c p) t -> p c t", p=128), q_st.t[:, 0:NH, :],
              reads=[act_b[32 + j] for j in range(NH)], writes=[b_q[i]], sem=q_st.sem)
        proj_rows(lambda o, k0, nk: w_in[k0 * 128:(k0 + nk) * 128, cfg.ok + o * 128: cfg.ok + (o + 1) * 128], DC,
                  rhs_fn, rb_fn, NH, mk_evac(k_st.t, 48, None, 1), banks=(0, 1, 2, 3))
        P.dma("sp", kTs[:, t0:t0 + T].rearrange("(c p) t -> p c t", p=128), k_st.t[:, 0:NH, :],
              reads=[act_b[48 + j] for j in range(NH)], writes=[b_k[i]], sem=k_st.sem)

        def ev_u(o, ps):
            P.op("dve", lambda E: E.tensor_copy(u_st.t[:, o, :], ps.t[:]), reads=[ps.b], writes=[act_b[64 + 2 * o], act_b[65 + 2 * o]])
        proj_rows(lambda o, k0, nk: w_in[k0 * 128:(k0 + nk) * 128, cfg.ou + o * 128: cfg.ou + (o + 1) * 128], DC,
                  rhs_fn, rb_fn, 8, ev_u, banks=(0, 1, 2, 3))
        P.dma("sp", uTs[:, t0:t0 + T].rearrange("(c p) t -> p c t", p=128), u_st.t[:, 0:8, :],
              reads=[act_b[64 + j] for j in range(16)], writes=[b_u[i]], sem=u_st.sem)

        nqc = 2 * MH
        o0 = 0
        while o0 < nqc:
            npc = min(6, nqc - o0)

            def ev_qc(o, ps, o0=o0):
                P.op("dve", lambda E: E.tensor_scalar(qc_st.t[:, o, :], ps.t[:], 256.0 ** -0.5, None, ALU.mult),
                     reads=[ps.b], writes=[act_b[80 + o]])
            proj_rows(lambda o, k0, nk, o0=o0: w_in[k0 * 128:(k0 + nk) * 128, cfg.oqc + (o0 + o) * 128: cfg.oqc + (o0 + o + 1) * 128], DC,
                      rhs_fn, rb_fn, npc, ev_qc, banks=(0, 1, 2, 3))
            P.dma("sp", qcTs[o0 * 128:(o0 + npc) * 128, t0:t0 + T].rearrange("(c p) t -> p c t", p=128), qc_st.t[:, 0:npc, :],
                  reads=[act_b[80 + j] for j in range(npc)], writes=[b_qc[i]], sem=qc_st.sem)
            o0 += npc

        KG = 8
        for vb in range(cfg.NAW // 512):
            for kg in range(DC // KG if DC >= KG else 1):
                nk = min(KG, DC)
                s = load_w(w_in[kg * KG * 128:(kg * KG + nk) * 128, cfg.ov + vb * 512: cfg.ov + (vb + 1) * 512], nk, wide=True)
                wv = s.tv
                for kk in range(nk):
                    k = kg * KG + kk
                    for tb in range(4):
                        ps = PS[4 + tb]
                        P.op("pe", (lambda ps, wv, kk, k, tb: lambda E: E.matmul(
                            ps.t[:], lhsT=H2[:, k, tb * 128:(tb + 1) * 128], rhs=wv[:, kk, :],
                            start=(k == 0), stop=(k == DC - 1)))(ps, wv, kk, k, tb),
                             reads=[s.b, act_b[k]], writes=[ps.b], inc=(k == DC - 1))
            for tb in range(4):
                P.op("dve", (lambda tb: lambda E: E.tensor_copy(q_st.t[:, tb, :], PS[4 + tb].t[:]))(tb),
                     reads=[PS[4 + tb].b], writes=[act_b[32 + tb]])
            P.dma("sp", vS[t0:t0 + T, vb * 512:(vb + 1) * 512].rearrange("(tb p) f -> p tb f", p=128), q_st.t[:, 0:4, :],
                  reads=[act_b[32 + j] for j in range(4)], writes=[b_v[i]], sem=q_st.sem)

    ATT = at("ATT", [128, 16, 512], BF16, R_ACT)
    OC = at("OC", [128, 8, 512], BF16, R_ACT + 16384)
    MIX = at("MIX", [128, 8, 512], BF16, R_ACT + 24576)
    MRG = at("MRG", [128, 32, 512], BF16, R_ACT + 32768)
    maskL = Reg(at("maskL", [8, 1024], BF16, R_ACT + 65536), None, P.new_sem("mkl"))
    KmT = at("KmT", [128, 8, 256], BF16, R_ACT + 67584)
    Vm = at("Vm", [128, 2, 1024], BF16, R_ACT + 71680)
    qcT = Reg(at("qcT", [128, 8, 512], BF16, R_ACT + 75776), None, P.new_sem("qcl"))
    wpool = Reg(at("wpool", [128, 8, 256], BF16, R_ACT + 83968), None, P.new_sem("wpl"))
    HM = at("HM", [128, 32, 256], BF16, R_Y)
    Ec = [at("Ec%d" % i, [128, 512], BF16, R_Y + 16384 + i * 1024) for i in range(2)]
    En = at("En", [128, 16, 512], BF16, R_Y)
    QTa = Reg(at("QTa", [128, 16, 512], BF16, R_Y + 16384), None, P.new_sem("qtl"))
    KTa = Reg(at("KTa", [128, 16, 1024], BF16, R_Y + 32768), None, P.new_sem("ktl"))
    Va = Reg(at("Va", [128, 8, 2048], BF16, R_W), None, P.new_sem("val"))
    bias_sl = [Reg(at("bsl%d" % i, [128, 1408], BF16, XS_BASE + (2 + 2 * i) * 2048), None, P.new_sem("bsl%d" % i)) for i in range(2)]
    bias_bufs = [[XS[2 + 2 * i].b, XS[3 + 2 * i].b] for i in range(2)]
    U_t = Reg(at("U_t", [128, 2, 528], F32, R_Y), None, P.new_sem("utl"))
    UM = at("UM", [128, 2, 528], F32, R_Y + 4608)
    PA = at("PA", [128, 2, 528], F32, R_Y + 9216)
    PB = at("PB", [128, 2, 528], F32, R_Y + 13824)
    PV = Reg(at("PVt", [128, 528], F32, R_Y + 18432), None, P.new_sem("pvl"))
    PI = Reg(at("PIt", [128, 4, 512], F32, R_Y + 20992), None, P.new_sem("pil"))
    PO = at("PO", [128, 8, 512], BF16, R_Y + 29696)

    def yb(lo, hi):
        return [y_b[i] for i in range(lo, hi)]

    def ab(lo, hi):
        return [act_b[i] for i in range(lo, hi)]

    def ws_all():
        return [w.b for w in wslots]

    def phaseBC(j):
        wctx.update(ph="B", pass0=(j == 0), key=0)
        te = (8 * j + 4) * 64
        tk = 8 * j * 64
        et = [b for b in (j, j + 1)]
        prenorm_from_dram(memT[j], [], 0, G_MEM, HM, lambda c: yb(c // 4, c // 4 + 1), n=256)

        def ev_km(o, ps):
            P.op("dve", lambda E: E.tensor_copy(KmT[:, o, :], ps.t[:, 0:256]), reads=[ps.b], writes=ab(66, 70))
        proj_rows(lambda o, k0, nk: w_kv[k0 * 128:(k0 + nk) * 128, o * 128:(o + 1) * 128], DC,
                  lambda k: HM[:, k, :], lambda k: yb(k // 4, k // 4 + 1), 2 * MH, ev_km, banks=(0, 1))
        KG = 8
        nvb = cfg.MW // 512
        for vb in range(nvb):
            for kg in range(max(1, DC // KG)):
                nk = min(KG, DC)
                s = load_w(w_kv[kg * KG * 128:(kg * KG + nk) * 128, cfg.MW + vb * 512: cfg.MW + (vb + 1) * 512], nk, wide=True)
                wv = s.tv
                for kk in range(nk):
                    k = kg * KG + kk
                    for mb in range(2):
                        ps = PS[2 + mb]
                        P.op("pe", (lambda ps, wv, kk, k, mb: lambda E: E.matmul(
                            ps.t[:], lhsT=HM[:, k, mb * 128:(mb + 1) * 128], rhs=wv[:, kk, :],
                            start=(k == 0), stop=(k == DC - 1)))(ps, wv, kk, k, mb),
                             reads=[s.b] + yb(k // 4, k // 4 + 1), writes=[ps.b], inc=(k == DC - 1))
            for mb in range(2):
                P.op("dve", (lambda mb, vb: lambda E: E.tensor_copy(Vm[:, mb, vb * 512:(vb + 1) * 512], PS[2 + mb].t[:]))(mb, vb),
                     reads=[PS[2 + mb].b], writes=ab(70, 74))
        P.dma("sp", qcT.t[:, 0:2 * MH, :], qcTs[:, te:te + T].rearrange("(c p) t -> p c t", p=128),
              reads=[b_qc[e] for e in et], writes=ab(74, 82), sem=qcT.sem)
        for hc in range(MH):
            for mb in range(2):
                ps = PS[mb]
                for dc in range(2):
                    P.op("pe", (lambda ps, hc, mb, dc: lambda E: E.matmul(
                        ps.t[:], lhsT=KmT[:, hc * 2 + dc, mb * 128:(mb + 1) * 128], rhs=qcT.t[:, hc * 2 + dc, :],
                        start=(dc == 0), stop=(dc == 1)))(ps, hc, mb, dc),
                         reads=ab(66, 70) + ab(74, 82), writes=[ps.b], inc=(dc == 1))
                P.op("act", (lambda ps, mb: lambda E: E.activation(Ec[mb][:], ps.t[:], AF.Exp))(ps, mb), reads=[ps.b], writes=[y_b[8]])
            for dc in range(2):
                ps = PS[2 + dc]
                for mb in range(2):
                    P.op("pe", (lambda ps, hc, mb, dc: lambda E: E.matmul(
                        ps.t[:], lhsT=Vm[:, mb, (hc * 2 + dc) * 128:(hc * 2 + dc + 1) * 128], rhs=Ec[mb][:],
                        start=(mb == 0), stop=(mb == 1)))(ps, hc, mb, dc),
                         reads=ab(70, 74) + [y_b[8]], writes=[ps.b], inc=(mb == 1))
            psd = PS[4]
            for mb in range(2):
                P.op("pe", (lambda mb: lambda E: E.matmul(psd.t[:], lhsT=ones_b.t[:], rhs=Ec[mb][:], start=(mb == 0), stop=(mb == 1)))(mb),
                     reads=[ones_b.b, y_b[8]], writes=[psd.b], inc=(mb == 1))
            ft = next_ft()
            P.op("dve", (lambda ft: lambda E: E.reciprocal(ft.t[:], psd.t[:]))(ft), reads=[psd.b], writes=[ft.b])
            for dc in range(2):
                P.op("dve", (lambda ft, hc, dc: lambda E: E.tensor_tensor(OC[:, hc * 2 + dc, :], PS[2 + dc].t[:], ft.t[:], ALU.mult))(ft, hc, dc),
                     reads=[PS[2 + dc].b, ft.b], writes=ab(16, 24))
        P.dma("pool", wpool.t[:], w_pool.rearrange("(g p) f -> p g f", p=128), [], ab(82, 86), wpool.sem)
        P.dma("sp", PV.t[:], pvalid_d[j], [], yb(9, 11), PV.sem)
        P.dma("sp", PI.t[:], pinv_d[j].rearrange("p (g t) -> p g t", t=512), [], yb(10, 15), PI.sem)
        for g in range(4):
            P.dma("sp", U_t.t[:], uTs[g * 256:(g + 1) * 256, te - 8:te + 520].rearrange("(c p) t -> p c t", p=128),
                  reads=[b_u[e] for e in et], writes=yb(0, 3), sem=U_t.sem)
            for c in range(2):
                P.op("dve", (lambda c: lambda E: E.tensor_tensor(UM[:, c, :], U_t.t[:, c, :], PV.t[:], ALU.mult))(c),
                     reads=yb(0, 3) + yb(9, 11), writes=yb(2, 5))
            src, dst = UM, PA
            P.op("dve", lambda E: E.tensor_tensor(PA[:, :, 1:528], UM[:, :, 0:527], UM[:, :, 1:528], ALU.add), reads=yb(2, 5), writes=yb(4, 7))
            cur_t, cur_b, oth_t, oth_b = PA, yb(4, 7), PB, yb(6, 9)
            lo, hi = 1, 528
            for lvl in range(g):
                sh = 1 << lvl
                nlo, nhi = lo + sh, hi - sh
                P.op("dve", (lambda cur_t, oth_t, nlo, nhi, sh: lambda E: E.tensor_tensor(
                    oth_t[:, :, nlo:nhi], cur_t[:, :, nlo - sh:nhi - sh], cur_t[:, :, nlo + sh:nhi + sh], ALU.add))(cur_t, oth_t, nlo, nhi, sh),
                     reads=cur_b, writes=oth_b)
                cur_t, cur_b, oth_t, oth_b = oth_t, oth_b, cur_t, cur_b
                lo, hi = nlo, nhi
            for c in range(2):
                P.op("dve", (lambda cur_t, c, g: lambda E: E.tensor_tensor(cur_t[:, c, 8:520], cur_t[:, c, 8:520], PI.t[:, g, :], ALU.mult))(cur_t, c, g),
                     reads=cur_b + yb(10, 15), writes=cur_b)
                P.op("dve", (lambda cur_t, c, g: lambda E: E.tensor_tensor(PO[:, g * 2 + c, :], cur_t[:, c, 8:520], U_t.t[:, c, 8:520], ALU.subtract))(cur_t, c, g),
                     reads=cur_b + yb(0, 3), writes=yb(14, 19))
            for oc in range(2):
                ps = PS[oc]
                for ic in range(2):
                    P.op("pe", (lambda ps, g, ic, oc: lambda E: E.matmul(
                        ps.t[:], lhsT=wpool.t[:, g * 2 + ic, oc * 128:(oc + 1) * 128], rhs=PO[:, g * 2 + ic, :],
                        start=(ic == 0), stop=(ic == 1)))(ps, g, ic, oc),
                         reads=ab(82, 86) + yb(14, 19), writes=[ps.b], inc=(ic == 1))
                P.op("dve", (lambda ps, g, oc: lambda E: E.tensor_scalar(MIX[:, g * 2 + oc, :], ps.t[:], pscale.t[:, g * 2 + oc: g * 2 + oc + 1], None, ALU.mult))(ps, g, oc),
                     reads=[ps.b, pscale.b], writes=ab(24, 32))
        P.dma("sp", QTa.t[:, 0:NH, :], qTs[:, te:te + T].rearrange("(c p) t -> p c t", p=128),
              reads=[b_q[e] for e in et], writes=yb(8, 16), sem=QTa.sem)
        P.dma("sp", KTa.t[:, 0:NH, :], kTs[:, tk:tk + 1024].rearrange("(c p) t -> p c t", p=128),
              reads=[b_k[e] for e in et], writes=yb(16, 32), sem=KTa.sem)
        P.dma("sp", Va.t[:, :, 0:cfg.NAW], vS[tk:tk + 1024, :].rearrange("(b p) f -> p b f", p=128),
              reads=[b_v[e] for e in et], writes=[wslots[i].b for i in range(4)], sem=Va.sem)
        P.dma("pool", maskL.t[:], maskL_d[j], [], ab(64, 66), maskL.sem)
        for h in range(NH):
            bs = bias_sl[h % 2]
            P.dma("pool", bs.t[:], biasT_d[h], [], bias_bufs[h % 2], bs.sem)
            eb = (h % 2) * 8
            for bp in range(8):
                ps = PS[bp % 4]
                j0 = (14 - 2 * bp) * 64
                P.op("pe", (lambda ps, h, bp: lambda E: E.matmul(ps.t[:], lhsT=KTa.t[:, h, bp * 128:(bp + 1) * 128], rhs=QTa.t[:, h, :],
                                                                start=True, stop=False))(ps, h, bp),
                     reads=yb(8, 32), writes=[ps.b], inc=False)
                P.op("pe", (lambda ps, bs, j0: lambda E: E.matmul(ps.t[:], lhsT=ident_b.t[:], rhs=bs.t[:, j0:j0 + 512],
                                                                 start=False, stop=False))(ps, bs, j0),
                     reads=[ident_b.b] + bias_bufs[h % 2], writes=[ps.b], inc=False)
                P.op("pe", (lambda ps, bp: lambda E: E.matmul(ps.t[:], lhsT=maskL.t[:, bp * 128:(bp + 1) * 128], rhs=SEL.t[:],
                                                             start=False, stop=True))(ps, bp),
                     reads=ab(64, 66) + [SEL.b], writes=[ps.b], inc=True)
                P.op("act", (lambda ps, eb, bp: lambda E: E.activation(En[:, eb + bp, :], ps.t[:], AF.Exp))(ps, eb, bp),
                     reads=[ps.b], writes=yb(eb // 2 + bp // 2, eb // 2 + bp // 2 + 1))
            pso, psd = PS[4 + h % 2], PS[6 + h % 2]
            for bp in range(8):
                P.op("pe", (lambda pso, h, bp, eb: lambda E: E.matmul(pso.t[:], lhsT=Va.t[:, bp, h * 128:(h + 1) * 128], rhs=En[:, eb + bp, :],
                                                                     start=(bp == 0), stop=(bp == 7)))(pso, h, bp, eb),
                     reads=[wslots[i].b for i in range(4)] + yb(eb // 2, eb // 2 + 4), writes=[pso.b], inc=(bp == 7))
            for bp in range(8):
                P.op("pe", (lambda psd, bp, eb: lambda E: E.matmul(psd.t[:], lhsT=ones_b.t[:], rhs=En[:, eb + bp, :],
                                                                  start=(bp == 0), stop=(bp == 7)))(psd, bp, eb),
                     reads=[ones_b.b] + yb(eb // 2, eb // 2 + 4), writes=[psd.b], inc=(bp == 7))
            ft = next_ft()
            P.op("dve", (lambda ft, psd: lambda E: E.reciprocal(ft.t[:], psd.t[:]))(ft, psd), reads=[psd.b], writes=[ft.b])
            P.op("dve", (lambda ft, pso, h: lambda E: E.tensor_tensor(ATT[:, h, :], pso.t[:], ft.t[:], ALU.mult))(ft, pso, h),
                 reads=[pso.b, ft.b], writes=ab(0, 16))
        prenorm_from_dram(x1T, [b_x1[e] for e in et], te, G_MIXPRE, Hb, hb_bufs)
        for m in range(DC):
            specs = [
                (w_in, cfg.og + 0 * D + m * 128, DC, lambda k: Hb[:, k, :], hb_bufs, None),
                (w_ao, m * 128, NH, lambda k: ATT[:, k, :], lambda k: ab(0, 16), 0),
                (w_in, cfg.og + 1 * D + m * 128, DC, lambda k: Hb[:, k, :], hb_bufs, None),
                (w_bo, m * 128, 8, lambda k: MIX[:, k, :], lambda k: ab(24, 32), 1),
                (w_in, cfg.og + 2 * D + m * 128, DC, lambda k: Hb[:, k, :], hb_bufs, None),
                (w_co, m * 128, 2 * MH, lambda k: OC[:, k, :], lambda k: ab(16, 24), 2),
            ]
            for br in range(3):
                psg, psy = PS[2 * br], PS[2 * br + 1]
                for (ps, spec) in ((psg, specs[2 * br]), (psy, specs[2 * br + 1])):
                    wsrc, col, nkc, rfn, rbf, _ = spec
                    s = load_w(wsrc[0:nkc * 128, col:col + 128], nkc)
                    for k in range(nkc):
                        P.op("pe", (lambda ps, s, k, rfn, nkc: lambda E: E.matmul(ps.t[:], lhsT=s.t[:, k, :], rhs=rfn(k),
                                                                              start=(k == 0), stop=(k == nkc - 1)))(ps, s, k, rfn, nkc),
                             reads=[s.b] + rbf(k), writes=[ps.b], inc=(k == nkc - 1))
                ft = next_ft()
                P.op("act", (lambda ft, psg, br, m: lambda E: E.activation(ft.t[:], psg.t[:], AF.Sigmoid,
                                                                         bias=bgate.t[:, br * DC + m: br * DC + m + 1]))(ft, psg, br, m),
                     reads=[psg.b, bgate.b], writes=[ft.b])
                if br == 0:
                    acc = next_ft()
                    P.op("dve", (lambda acc, ft, psy: lambda E: E.tensor_tensor(acc.t[:], ft.t[:], psy.t[:], ALU.mult))(acc, ft, psy),
                         reads=[ft.b, psy.b], writes=[acc.b])
                else:
                    P.op("dve", (lambda ft, psy: lambda E: E.tensor_tensor(ft.t[:], ft.t[:], psy.t[:], ALU.mult))(ft, psy),
                         reads=[ft.b, psy.b], writes=[ft.b])
                    if br == 1:
                        P.op("dve", (lambda acc, ft: lambda E: E.tensor_tensor(acc.t[:], acc.t[:], ft.t[:], ALU.add))(acc, ft),
                             reads=[ft.b, acc.b], writes=[acc.b])
                    else:
                        P.op("dve", (lambda acc, ft, m: lambda E: E.tensor_tensor(MRG[:, m, :], acc.t[:], ft.t[:], ALU.add))(acc, ft, m),
                             reads=[ft.b, acc.b], writes=ab(32 + m, 33 + m))
        proj_rows(lambda o, k0, nk: w_o[k0 * 128:(k0 + nk) * 128, o * 128:(o + 1) * 128], DC,
                  lambda k: MRG[:, k, :], lambda k: ab(32 + k, 33 + k), DC, evac_to_Y_with_stats, banks=(4, 5))
        residual_finish(x1T, [b_x1[e] for e in et], te, lambda c: gcol(G_MIXPOST, c))
        store_Y(x2T, [b_x2[j]], j * T)
        ffn_stage(x2T, [b_x2[j]], j * T, w2g, w2u, w2d, G_F2PRE, 1)
        stats_finish()
        for c in range(DC):
            P.op("dve", (lambda c: lambda E: E.scalar_tensor_tensor(Yf[:, c, :], Yf[:, c, :], gcol(G_FINAL, c), RSTD.t[:], ALU.mult, ALU.mult))(c),
                 reads=[y_b[c], RSTD.b, gains.b], writes=[y_b[c]])
        store_Y(yT, [b_y[j]], j * T)

    for f in range(FC):
        conv_list.append((keyF2 + 2 * f, w2g[:, f * 128:(f + 1) * 128], DC))
        conv_list.append((keyF2 + 2 * f + 1, w2u[:, f * 128:(f + 1) * 128], DC))
    for i in range(NEXT):
        phaseA(i)
    for j in range(own):
        phaseBC(j)
    P.final_wait("sp", [sem_stY])

    with nc.Block() as block:
        block.sync(P.replay("sp"))
        block.gpsimd(P.replay("pool"))
        block.tensor(P.replay("pe"))
        block.scalar(P.replay("act"))
        block.vector(P.replay("dve"))
    return nc


def _tile_table(cfg):
    tab = []
    for s, n in enumerate(cfg.seq_tiles):
        for i in range(n):
            tab.append((s, i, n))
    return tab


def prepare_inputs(cfg, inp):
    DC, D, T, own, NEXT, NH = cfg.DC, cfg.D, cfg.T, cfg.own, cfg.next, cfg.NH
    f32 = np.float32
    xs = [np.asarray(inp["x_prompt"], f32)[b] for b in range(inp["x_prompt"].shape[0])] + \
         [np.asarray(inp["x_sample"], f32)[b] for b in range(inp["x_sample"].shape[0])]
    mems = [np.asarray(inp["mem_prompt"], f32)[b] for b in range(inp["mem_prompt"].shape[0])] + \
           [np.asarray(inp["mem_sample"], f32)[b] for b in range(inp["mem_sample"].shape[0])]
    assert len(xs) == len(cfg.seq_tiles)
    tab = _tile_table(cfg)
    G = len(tab)
    xall = np.concatenate(xs, axis=0)
    xallT = np.zeros((D, (G + 2) * T), f32)
    xallT[:, T:T + G * T] = xall.T
    memsT = [np.ascontiguousarray(m.T) for m in mems]

    def gvec(v):
        return np.asarray(v, f32).reshape(DC, 128).T

    gains = np.concatenate([gvec(inp[k][0]) for k in
                            ("g_ffn1_pre", "g_ffn1_post", "g_mix_pre", "g_mem", "g_mix_post", "g_ffn2_pre", "g_ffn2_post", "g_final")], axis=1)
    bgate = np.concatenate([gvec(inp["b_gate"][0][i]) for i in range(3)], axis=1)
    pscale = np.asarray(inp["pool_scale"][0], f32).reshape(8, 128).T
    rpb = np.asarray(inp["rpb"][0], f32)
    rpb_ext = np.concatenate([rpb.reshape(NH, -1), np.full((NH, 1), NEG, f32)], axis=1)
    NEGI = 15 * 31
    cols = np.arange(64)
    col_start = np.clip(cols - 8, 0, 48)
    idx = np.full((128, 22, 64), NEGI, np.int64)
    for bprime in range(2):
        for kc in range(64):
            p = bprime * 64 + kc
            for jj in range(22):
                ro = (14 - jj) + bprime + 3
                if ro < 0 or ro > 14:
                    continue
                ok = (kc >= col_start) & (kc < col_start + 16)
                co = kc - cols + 15
                idx[p, jj, ok] = ro * 31 + co[ok]
    biasT = np.ascontiguousarray(rpb_ext[:, idx.reshape(-1)].reshape(NH, 128, 22 * 64))
    sel = np.zeros((8, 512), f32)
    for a in range(8):
        sel[a, a * 64:(a + 1) * 64] = 1.0
    ident = np.eye(128, dtype=f32)
    w_pool = np.asarray(inp["w_pool"][0], f32).reshape(4 * 256, 256)

    shared = {
        "w1_gate": np.asarray(inp["w1_gate"][0], f32), "w1_up": np.asarray(inp["w1_up"][0], f32), "w1_down": np.asarray(inp["w1_down"][0], f32),
        "w2_gate": np.asarray(inp["w2_gate"][0], f32), "w2_up": np.asarray(inp["w2_up"][0], f32), "w2_down": np.asarray(inp["w2_down"][0], f32),
        "w_in": np.asarray(inp["w_in"][0], f32), "w_mem_kv": np.asarray(inp["w_mem_kv"][0], f32),
        "w_a_out": np.asarray(inp["w_a_out"][0], f32), "w_b_out": np.asarray(inp["w_b_out"][0], f32), "w_c_out": np.asarray(inp["w_c_out"][0], f32),
        "w_o": np.asarray(inp["w_o"][0], f32), "w_pool": w_pool,
        "gains": np.ascontiguousarray(gains), "bgate": np.ascontiguousarray(bgate), "pscale": np.ascontiguousarray(pscale),
        "biasT": biasT, "sel": sel, "ident": ident,
    }
    in_maps = []
    for c in range(cfg.n_cores):
        g0 = c * own
        tstart = T + g0 * T - 256
        xT = np.ascontiguousarray(xallT[:, tstart:tstart + NEXT * T])
        memT = np.stack([memsT[tab[g0 + j][0]] for j in range(own)], axis=0)
        maskL = np.full((own, 8, 8, 2, 64), NEG, f32)
        pvalid = np.zeros((own, 128, 528), f32)
        pinv = np.zeros((own, 128, 4, 512), f32)
        for j in range(own):
            s, i, n = tab[g0 + j]
            first, last = (i == 0), (i == n - 1)
            for a in range(8):
                rs = a - 4
                if first:
                    rs = max(rs, 0)
                if last:
                    rs = min(rs, 0)
                for b in range(rs + 4, rs + 12):
                    maskL[j, a, b // 2, b % 2, :] = 0.0
            Ls, Le = -i * T, (n - i) * T
            ii = np.arange(528) - 8
            pvalid[j, :, :] = ((ii >= Ls) & (ii < Le)).astype(f32)[None, :]
            t = np.arange(512)
            for g, w in enumerate((2, 4, 8, 16)):
                lo = np.maximum(t - w // 2, Ls)
                hi = np.minimum(t + w // 2, Le)
                pinv[j, :, g, :] = (1.0 / (hi - lo).astype(f32))[None, :]
        m = dict(shared)
        m.update({"xT": xT, "memT": np.ascontiguousarray(memT), "maskL": np.ascontiguousarray(maskL.reshape(own, 8, 1024)),
                  "pvalid": pvalid, "pinv": np.ascontiguousarray(pinv.reshape(own, 128, 2048))})
        in_maps.append(m)
    return in_maps


def assemble_outputs(cfg, results, inp):
    T, own = cfg.T, cfg.own
    yall = np.concatenate([np.asarray(r["yT"]) for r in results], axis=1)
    y = np.ascontiguousarray(yall.T)
    outs = []
    off = 0
    seq_i = 0
    for key in ("x_prompt", "x_sample"):
        B = inp[key].shape[0]
        L = inp[key].shape[1]
        parts = []
        for b in range(B):
            parts.append(y[off:off + L])
            off += L
            seq_i += 1
        outs.append(np.stack(parts, axis=0).astype(np.float32))
    return tuple(outs)


_CACHE = {}


def run(cfg, inp):
    in_maps = prepare_inputs(cfg, inp)
    key = (cfg.DC, cfg.FC, cfg.NH, cfg.MH, cfg.seq_tiles)
    nc = build_program(cfg)
    res = run_bass_kernel_spmd(nc, in_maps, core_ids=list(range(cfg.n_cores)))
    return assemble_outputs(cfg, res.results, inp)


def kernel(**inputs):
    cfg = Cfg()
    return run(cfg, inputs)
```

```python
import numpy as np
import concourse.bass as bass
import concourse.mybir as mybir
from concourse.bass_utils import run_bass_kernel_spmd

F32 = mybir.dt.float32
BF16 = mybir.dt.bfloat16
AF = mybir.ActivationFunctionType
ALU = mybir.AluOpType
NEG = -30000.0
EPS = 1e-6
SAME_ENGINE_SYNC = True


class Cfg:
    def __init__(self, DC=32, FC=86, NH=16, MH=4, seq_tiles=(32, 8, 8), n_cores=8):
        self.DC, self.FC, self.NH, self.MH = DC, FC, NH, MH
        self.D, self.DFF = 128 * DC, 128 * FC
        self.NAW, self.PW, self.MW = NH * 128, 1024, MH * 256
        self.INW = 3 * self.NAW + self.PW + self.MW + 3 * self.D
        self.seq_tiles = tuple(seq_tiles)
        self.n_cores = n_cores
        tot = sum(seq_tiles)
        assert tot % n_cores == 0
        self.own = tot // n_cores
        self.next = self.own + 1
        self.T = 512
        self.NMEM = 256
        self.oq, self.ok, self.ov = 0, self.NAW, 2 * self.NAW
        self.ou = 3 * self.NAW
        self.oqc = self.ou + self.PW
        self.og = self.oqc + self.MW


class Sem:
    def __init__(self, handle, name):
        self.h = handle
        self.name = name
        self.count = 0


class Buf:
    __slots__ = ("w", "r", "name")

    def __init__(self, name=""):
        self.w = None
        self.r = []
        self.name = name


class Engine:
    def __init__(self, name, sem):
        self.name = name
        self.sem = sem
        self.ops = []
        self.seen = {}


class Prog:
    def __init__(self, nc):
        self.nc = nc
        self.E = {}
        self.nsem = 0

    def new_sem(self, name):
        self.nsem += 1
        return Sem(self.nc.alloc_semaphore(name), name)

    def add_engine(self, name):
        self.E[name] = Engine(name, self.new_sem("e_" + name))

    def _waits(self, eng, reads, writes, skip_self):
        need = {}
        for b in reads:
            if b.w is not None:
                s, v = b.w
                if need.get(s, 0) < v:
                    need[s] = v
        for b in writes:
            if b.w is not None:
                s, v = b.w
                if need.get(s, 0) < v:
                    need[s] = v
            for (s, v) in b.r:
                if need.get(s, 0) < v:
                    need[s] = v
        waits = []
        for s, v in need.items():
            if s is eng.sem and skip_self:
                continue
            if eng.seen.get(s, 0) < v:
                eng.seen[s] = v
                waits.append((s, v))
        return waits

    def op(self, ename, fn, reads=(), writes=(), inc=True):
        eng = self.E[ename]
        skip_self = (ename == "pe") or (not SAME_ENGINE_SYNC)
        waits = self._waits(eng, reads, writes, skip_self)
        if inc:
            eng.sem.count += 1
            stamp = (eng.sem, eng.sem.count)
        else:
            stamp = (eng.sem, eng.sem.count + 1)
        for b in writes:
            b.w = stamp
            b.r = []
        for b in reads:
            if not b.r or b.r[-1] != stamp:
                b.r.append(stamp)
        eng.ops.append((waits, fn, eng.sem if inc else None, 1))

    def dma(self, qname, out, in_, reads, writes, sem):
        eng = self.E[qname]
        waits = self._waits(eng, reads, writes, False)
        if sem.count > 0 and eng.seen.get(sem, 0) < sem.count:
            eng.seen[sem] = sem.count
            waits.append((sem, sem.count))
        sem.count += 16
        stamp = (sem, sem.count)
        for b in writes:
            b.w = stamp
            b.r = []
        for b in reads:
            b.r.append(stamp)

        def fn(E, out=out, in_=in_):
            return E.dma_start(out=out, in_=in_)

        eng.ops.append((waits, fn, sem, 16))

    def final_wait(self, qname, sems):
        eng = self.E[qname]
        waits = [(s, s.count) for s in sems if s.count > 0]
        eng.ops.append((waits, None, None, 0))

    def replay(self, ename):
        ops = self.E[ename].ops

        def run(E):
            for (waits, fn, sem, amt) in ops:
                for (s, v) in waits:
                    E.wait_ge(s.h, v)
                if fn is None:
                    continue
                ins = fn(E)
                if sem is not None:
                    ins.then_inc(sem.h, amt)

        return run


class Reg:
    def __init__(self, t, b=None, sem=None):
        self.t = t
        self.b = b if b is not None else Buf()
        self.sem = sem


def build_program(cfg):
    nc = bass.Bass("TRN2", target_bir_lowering=False)
    P = Prog(nc)
    for n in ("pe", "act", "dve", "pool", "sp"):
        P.add_engine(n)
    DC, FC, NH, MH, D, DFF, T = cfg.DC, cfg.FC, cfg.NH, cfg.MH, cfg.D, cfg.DFF, cfg.T
    own, NEXT = cfg.own, cfg.next
    NTOK = NEXT * T

    def din(name, shape, dt=F32):
        return nc.dram_tensor(name, list(shape), dt, kind="ExternalInput").ap()

    def dscr(name, shape, dt):
        return nc.dram_tensor(name, list(shape), dt).ap()

    xT = din("xT", [D, NTOK])
    memT = din("memT", [own, D, 256])
    w1g, w1u, w1d = din("w1_gate", [D, DFF]), din("w1_up", [D, DFF]), din("w1_down", [DFF, D])
    w2g, w2u, w2d = din("w2_gate", [D, DFF]), din("w2_up", [D, DFF]), din("w2_down", [DFF, D])
    w_in = din("w_in", [D, cfg.INW])
    w_kv = din("w_mem_kv", [D, 2 * cfg.MW])
    w_ao, w_bo, w_co = din("w_a_out", [cfg.NAW, D]), din("w_b_out", [cfg.PW, D]), din("w_c_out", [cfg.MW, D])
    w_o = din("w_o", [D, D])
    w_pool = din("w_pool", [4 * 256, 256])
    NG = 8
    gains_d = din("gains", [128, NG * DC])
    bgate_d = din("bgate", [128, 3 * DC])
    pscale_d = din("pscale", [128, 8])
    biasT_d = din("biasT", [NH, 128, 22 * 64])
    maskL_d = din("maskL", [own, 8, 1024])
    pvalid_d = din("pvalid", [own, 128, 528])
    pinv_d = din("pinv", [own, 128, 4 * 512])
    sel_d = din("sel", [8, 512])
    ident_d = din("ident", [128, 128])
    yT = nc.dram_tensor("yT", [D, own * T], F32, kind="ExternalOutput").ap()

    x1T = dscr("x1T", [D, NTOK], F32)
    x2T = dscr("x2T", [D, own * T], F32)
    qTs = dscr("qTs", [cfg.NAW, NTOK], BF16)
    kTs = dscr("kTs", [cfg.NAW, NTOK], BF16)
    vS = dscr("vS", [NTOK, cfg.NAW], BF16)
    uTs = dscr("uTs", [cfg.PW, NTOK], F32)
    qcTs = dscr("qcTs", [cfg.MW, NTOK], BF16)
    h2Ts = dscr("h2Ts", [D, NTOK], BF16)
    b_h2 = [Buf() for _ in range(NEXT)]
    b_x1 = [Buf() for _ in range(NEXT)]
    b_q = [Buf() for _ in range(NEXT)]
    b_k = [Buf() for _ in range(NEXT)]
    b_v = [Buf() for _ in range(NEXT)]
    b_u = [Buf() for _ in range(NEXT)]
    b_qc = [Buf() for _ in range(NEXT)]
    b_x2 = [Buf() for _ in range(own)]
    b_y = [Buf() for _ in range(own)]

    ARENA = 210944
    arena = nc.alloc_sbuf_tensor("arena", [128, ARENA // 4], F32)
    base = nc.lookup_mloc(arena).addr

    def at(name, shape, dt, off):
        return nc.alloc_sbuf_tensor_at(name, list(shape), dt, offset=off)

    R_ACT = base
    R_Y = R_ACT + 86 * 1024
    R_W = R_Y + 65536
    NWS = 4
    NXS = 6
    R_M = R_W + NWS * 8192

    act_b = [Buf("act%d" % i) for i in range(86)]
    y_b = [Buf("y%d" % i) for i in range(32)]
    aT = at("aT", [128, 86, 512], BF16, R_ACT)
    Yf = at("Yf", [128, 32, 512], F32, R_Y)
    Hb = at("Hb", [128, 32, 512], BF16, R_Y)
    H2 = at("H2", [128, 32, 512], BF16, R_ACT)

    def hb_bufs(c):
        return [y_b[c // 2]]

    wslots = []
    for i in range(NWS):
        t = at("ws%d" % i, [128, 32, 128], BF16, R_W + i * 8192)
        r = Reg(t, Buf("ws%d" % i), P.new_sem("ws%d" % i))
        r.tv = at("wsv%d" % i, [128, 8, 512], BF16, R_W + i * 8192)
        r.sem2 = P.new_sem("wst%d" % i)
        wslots.append(r)
    wctr = [0]

    def next_wslot():
        s = wslots[wctr[0] % NWS]
        wctr[0] += 1
        return s

    off = R_M
    XS = []
    XS_BASE = off
    for i in range(NXS):
        XS.append(Reg(at("xs%d" % i, [128, 512], F32, off), Buf(), P.new_sem("xs%d" % i)))
        off = (off + 2048 + 31) // 32 * 32
    xsctr = [0]
    FT = []
    for i in range(3):
        FT.append(Reg(at("ft%d" % i, [128, 512], F32, off), Buf()))
        off = (off + 2048 + 31) // 32 * 32
    ftctr = [0]

    def next_ft():
        s = FT[ftctr[0] % 3]
        ftctr[0] += 1
        return s

    RSTD = Reg(at("rstd", [128, 512], F32, off)); off = (off + 2048 + 31) // 32 * 32
    ones_f = Reg(at("ones_f", [128, 128], F32, off)); off = (off + 512 + 31) // 32 * 32
    ones_b = Reg(at("ones_b", [128, 128], BF16, off)); off = (off + 256 + 31) // 32 * 32
    ident_b = Reg(at("ident_b", [128, 128], BF16, off), sem=P.new_sem("identl")); off = (off + 256 + 31) // 32 * 32
    SEL = Reg(at("sel", [8, 512], BF16, off), sem=P.new_sem("sell")); off = (off + 1024 + 31) // 32 * 32
    gains = Reg(at("gains", [128, NG * DC], F32, off), sem=P.new_sem("gl")); off = (off + NG * DC * 4 + 31) // 32 * 32
    hgain = Reg(at("hgain", [128, 2 * DC], F32, off)); off = (off + 2 * DC * 4 + 31) // 32 * 32
    bgate = Reg(at("bgate", [128, 3 * DC], F32, off), sem=P.new_sem("bgl")); off = (off + 3 * DC * 4 + 31) // 32 * 32
    pscale = Reg(at("pscale", [128, 8], F32, off), sem=P.new_sem("psl")); off = (off + 32 + 31) // 32 * 32
    epsb = Reg(at("epsb", [128, 8], F32, off)); off = (off + 32 + 31) // 32 * 32
    assert off <= base + ARENA, (off - base, ARENA)
    G_F1PRE, G_F1POST, G_MIXPRE, G_MEM, G_MIXPOST, G_F2PRE, G_F2POST, G_FINAL = range(8)

    def gcol(gi, c):
        return gains.t[:, gi * DC + c: gi * DC + c + 1]

    PS = [Reg(nc.alloc_psum_tensor("ps%d" % i, [128, 512], F32), Buf("ps%d" % i)) for i in range(8)]

    cdiv = lambda a, b: (a + b - 1) // b
    nA = 2 * FC + DC * cdiv(FC, 32) + (2 * NH + 8 + 2 * MH) * cdiv(DC, 32) + (cfg.NAW // 512) * max(1, DC // 8)
    nB = 2 * MH * cdiv(DC, 32) + (cfg.MW // 512) * max(1, DC // 8) + 6 * DC + DC * cdiv(DC, 32) + 2 * FC + DC * cdiv(FC, 32)
    WCH = 200
    wscr = {ph: [dscr("wscr%s%d" % (ph, i), [min(WCH, n - i * WCH), 128, 4096], BF16) for i in range(cdiv(n, WCH))]
            for ph, n in (("A", nA), ("B", nB))}
    wscr_b = {"A": [Buf() for _ in range(nA)], "B": [Buf() for _ in range(nB)]}
    wctx = {"ph": "A", "pass0": True, "key": 0, "cc": 0}
    keyF2 = 2 * MH * cdiv(DC, 32) + (cfg.MW // 512) * max(1, DC // 8) + 6 * DC + DC * cdiv(DC, 32)
    conv_list = []
    preconv = set()
    CONV_EVERY = 5

    def maybe_convert():
        if wctx["ph"] != "A" or wctx["pass0"] or not conv_list:
            return
        wctx["cc"] += 1
        if wctx["cc"] % CONV_EVERY:
            return
        key, src_ap, nk = conv_list.pop(0)
        s = next_wslot()
        dst = s.t[:, 0:nk, :]
        img = wscr["B"][key // WCH][key % WCH, :, 0:nk * 128].rearrange("p (c f) -> p c f", f=128)
        P.dma("pool", dst, src_ap.rearrange("(c p) f -> p c f", p=128), reads=[], writes=[s.b], sem=s.sem)
        P.dma("sp", img, dst, reads=[s.b], writes=[wscr_b["B"][key]], sem=s.sem2)
        preconv.add(key)

    def load_w(src_ap, nk, wide=False):
        s = next_wslot()
        width = 512 if wide else 128
        dst = s.tv[:, 0:nk, :] if wide else s.t[:, 0:nk, :]
        ph, key = wctx["ph"], wctx["key"]
        wctx["key"] += 1
        img = wscr[ph][key // WCH][key % WCH, :, 0:nk * width].rearrange("p (c f) -> p c f", f=width)
        if wctx["pass0"] and not (ph == "B" and key in preconv):
            P.dma("pool", dst, src_ap.rearrange("(c p) f -> p c f", p=128), reads=[], writes=[s.b], sem=s.sem)
            P.dma("sp", img, dst, reads=[s.b], writes=[wscr_b[ph][key]], sem=s.sem2)
        else:
            P.dma("pool", dst, img, reads=[wscr_b[ph][key]], writes=[s.b], sem=s.sem)
        return s

    P.op("dve", lambda E: E.memset(ones_f.t[:], 1.0), writes=[ones_f.b])
    P.op("dve", lambda E: E.memset(ones_b.t[:], 1.0), writes=[ones_b.b])
    P.op("dve", lambda E: E.memset(epsb.t[:], EPS), writes=[epsb.b])
    P.dma("pool", ident_b.t[:], ident_d[:, :], [], [ident_b.b], ident_b.sem)
    P.dma("pool", SEL.t[:], sel_d[:, :], [], [SEL.b], SEL.sem)
    P.dma("sp", gains.t[:], gains_d[:, :], [], [gains.b], gains.sem)
    P.dma("sp", bgate.t[:], bgate_d[:, :], [], [bgate.b], bgate.sem)
    P.dma("sp", pscale.t[:], pscale_d[:, :], [], [pscale.b], pscale.sem)
    P.op("dve", lambda E: E.tensor_scalar(hgain.t[:, 0:DC], gains.t[:, G_F1POST * DC:(G_F1POST + 1) * DC], 0.5, None, ALU.mult),
         reads=[gains.b], writes=[hgain.b])
    P.op("dve", lambda E: E.tensor_scalar(hgain.t[:, DC:2 * DC], gains.t[:, G_F2POST * DC:(G_F2POST + 1) * DC], 0.5, None, ALU.mult),
         reads=[gains.b], writes=[hgain.b])

    STAT = PS[6]

    def stats_sq(src_ap, src_bufs, n=512, on_act=False):
        ft = next_ft()
        if on_act:
            P.op("act", lambda E: E.activation(ft.t[:, 0:n], src_ap, AF.Square), reads=src_bufs, writes=[ft.b])
        else:
            P.op("dve", lambda E: E.tensor_tensor(ft.t[:, 0:n], src_ap, src_ap, ALU.mult), reads=src_bufs, writes=[ft.b])
        return ft

    def stats_mm(ft, c, nchunks, n=512):
        P.op("pe", lambda E: E.matmul(STAT.t[:, 0:n], lhsT=ones_f.t[:], rhs=ft.t[:, 0:n], start=(c == 0), stop=(c == nchunks - 1)),
             reads=[ones_f.b, ft.b], writes=[STAT.b], inc=True)

    def stats_accum(src_ap, src_bufs, c, nchunks, n=512, on_act=False):
        stats_mm(stats_sq(src_ap, src_bufs, n, on_act), c, nchunks, n)

    pend = []

    def flush_pend():
        while pend:
            pend.pop(0)()

    def stats_finish(n=512):
        P.op("act", lambda E: E.activation(RSTD.t[:, 0:n], STAT.t[:, 0:n], AF.Sqrt, bias=epsb.t[:, 0:1], scale=1.0 / D),
             reads=[STAT.b, epsb.b], writes=[RSTD.b])
        P.op("dve", lambda E: E.reciprocal(RSTD.t[:, 0:n], RSTD.t[:, 0:n]),
             reads=[RSTD.b], writes=[RSTD.b])

    def xs_load(src_ap, src_bufs, n=512):
        s = XS[xsctr[0] % NXS]
        xsctr[0] += 1
        P.dma("sp", s.t[:, 0:n], src_ap, reads=src_bufs, writes=[s.b], sem=s.sem)
        return s

    def prenorm_from_dram(src, src_bufs, t0, gi, dst, dst_bufs_fn, n=512, nch=None):
        nch = DC if nch is None else nch
        for c in range(nch):
            s = xs_load(src[c * 128:(c + 1) * 128, t0:t0 + n], src_bufs, n)
            stats_accum(s.t[:, 0:n], [s.b], c, nch, n, on_act=True)
        stats_finish(n)
        for c in range(nch):
            s = xs_load(src[c * 128:(c + 1) * 128, t0:t0 + n], src_bufs, n)
            P.op("dve", (lambda c, s: lambda E: E.scalar_tensor_tensor(dst[:, c, 0:n], s.t[:, 0:n], gcol(gi, c), RSTD.t[:, 0:n],
                                                                      ALU.mult, ALU.mult))(c, s),
                 reads=[s.b, RSTD.b, gains.b], writes=dst_bufs_fn(c))

    def prenorm_from_Y(gi, dst, dst_bufs_fn, stats_done=False):
        for c in range(DC):
            if not stats_done:
                stats_accum(Yf[:, c, :], [y_b[c]], c, DC)
        stats_finish()
        for c in range(DC):
            P.op("dve", (lambda c: lambda E: E.scalar_tensor_tensor(dst[:, c, :], Yf[:, c, :], gcol(gi, c), RSTD.t[:],
                                                                   ALU.mult, ALU.mult))(c),
                 reads=[y_b[c], RSTD.b, gains.b], writes=dst_bufs_fn(c))

    def proj_rows(wsrc_fn, kchunks, rhs_fn, rhs_bufs_fn, nout, evac_fn, banks):
        for o in range(nout):
            ps = PS[banks[o % len(banks)]]
            k0 = 0
            first = True
            while k0 < kchunks:
                nk = min(32, kchunks - k0)
                s = load_w(wsrc_fn(o, k0, nk), nk)
                for kk in range(nk):
                    k = k0 + kk
                    last = (k == kchunks - 1)
                    rhs = rhs_fn(k)
                    P.op("pe", (lambda ps, s, kk, rhs, first, last: lambda E: E.matmul(
                        ps.t[:, 0:rhs.shape[-1]], lhsT=s.t[:, kk, :], rhs=rhs, start=first, stop=last))(ps, s, kk, rhs, first, last),
                         reads=[s.b] + rhs_bufs_fn(k), writes=[ps.b], inc=last)
                    first = False
                k0 += nk
            flush_pend()
            evac_fn(o, ps)
        flush_pend()

    def residual_finish(src, src_bufs, t0, gain_ap_fn, accum_stats=False):
        stats_finish()
        for c in range(DC):
            s = xs_load(src[c * 128:(c + 1) * 128, t0:t0 + 512], src_bufs)
            P.op("dve", (lambda c: lambda E: E.scalar_tensor_tensor(Yf[:, c, :], Yf[:, c, :], gain_ap_fn(c), RSTD.t[:],
                                                                   ALU.mult, ALU.mult))(c),
                 reads=[y_b[c], RSTD.b, gains.b, hgain.b], writes=[y_b[c]])
            P.op("dve", (lambda c, s: lambda E: E.tensor_tensor(Yf[:, c, :], Yf[:, c, :], s.t[:], ALU.add))(c, s),
                 reads=[y_b[c], s.b], writes=[y_b[c]])
            if accum_stats:
                stats_accum(Yf[:, c, :], [y_b[c]], c, DC, on_act=True)

    def evac_to_Y_with_stats(o, ps):
        P.op("act", lambda E: E.copy(Yf[:, o, :], ps.t[:]), reads=[ps.b], writes=[y_b[o]])
        ft = stats_sq(Yf[:, o, :], [y_b[o]])
        pend.append(lambda: stats_mm(ft, o, DC))

    sem_stY = P.new_sem("stY")

    def store_Y(dst, dst_bufs, t0):
        P.dma("sp", dst[:, t0:t0 + 512].rearrange("(c p) t -> p c t", p=128), Yf[:, 0:DC, :],
              reads=[y_b[c] for c in range(DC)], writes=dst_bufs, sem=sem_stY)

    def ffn_stage(src, src_bufs, t0, wg, wu, wd, g_pre, hg_off):
        prenorm_from_dram(src, src_bufs, t0, g_pre, Hb, hb_bufs)
        for f in range(FC):
            sg = load_w(wg[:, f * 128:(f + 1) * 128], DC)
            maybe_convert()
            su = load_w(wu[:, f * 128:(f + 1) * 128], DC)
            maybe_convert()
            pg, pu = PS[f % 2], PS[2 + f % 2]
            for (s, ps) in ((sg, pg), (su, pu)):
                for c in range(DC):
                    P.op("pe", (lambda ps, s, c: lambda E: E.matmul(ps.t[:], lhsT=s.t[:, c, :], rhs=Hb[:, c, :],
                                                                   start=(c == 0), stop=(c == DC - 1)))(ps, s, c),
                         reads=[s.b] + hb_bufs(c), writes=[ps.b], inc=(c == DC - 1))
            ft = next_ft()
            P.op("act", (lambda ft, pg: lambda E: E.activation(ft.t[:], pg.t[:], AF.Silu))(ft, pg), reads=[pg.b], writes=[ft.b])
            P.op("dve", (lambda ft, pu, f: lambda E: E.tensor_tensor(aT[:, f, :], ft.t[:], pu.t[:], ALU.mult))(ft, pu, f),
                 reads=[ft.b, pu.b], writes=[act_b[f]])
        proj_rows(lambda o, k0, nk: wd[k0 * 128:(k0 + nk) * 128, o * 128:(o + 1) * 128], FC,
                  lambda k: aT[:, k, :], lambda k: [act_b[k]], DC, evac_to_Y_with_stats, banks=(4, 5))
        residual_finish(src, src_bufs, t0, lambda c: hgain.t[:, hg_off * DC + c: hg_off * DC + c + 1], accum_stats=True)

    q_st = Reg(at("q_st", [128, 16, 512], BF16, R_ACT + 32768), None, P.new_sem("qst"))
    k_st = Reg(at("k_st", [128, 16, 512], BF16, R_ACT + 49152), None, P.new_sem("kst"))
    u_st = Reg(at("u_st", [128, 8, 512], F32, R_ACT + 65536), None, P.new_sem("ust"))
    qc_st = Reg(at("qc_st", [128, 6, 512], BF16, R_ACT + 81920), None, P.new_sem("qcst"))

    def h2_bufs(c):
        return [act_b[c]]

    sem_h2 = P.new_sem("h2st")
    sem_hb = P.new_sem("hbld")

    def phaseA(i):
        t0 = i * T
        wctx.update(ph="A", pass0=(i == 0), key=0)
        ffn_stage(xT, [], t0, w1g, w1u, w1d, G_F1PRE, 0)
        store_Y(x1T, [b_x1[i]], t0)
        prenorm_from_Y(G_MIXPRE, H2, h2_bufs, stats_done=True)
        P.dma("sp", h2Ts[:, t0:t0 + T].rearrange("(c p) t -> p c t", p=128), H2[:, 0:DC, :],
              reads=[act_b[c] for c in range(DC)], writes=[b_h2[i]], sem=sem_h2)

        def mk_evac(stage, bufidx0, scale, width):
            def ev(o, ps):
                bl = [act_b[bufidx0 + (o * width) // 1 + j] for j in range(width)]
                if scale is None:
                    P.op("dve", lambda E: E.tensor_copy(stage[:, o, :], ps.t[:]), reads=[ps.b], writes=bl)
                else:
                    P.op("dve", lambda E: E.tensor_scalar(stage[:, o, :], ps.t[:], scale, None, ALU.mult), reads=[ps.b], writes=bl)
            return ev

        rhs_fn = lambda k: H2[:, k, :]
        rb_fn = lambda k: [act_b[k]]
        proj_rows(lambda o, k0, nk: w_in[k0 * 128:(k0 + nk) * 128, cfg.oq + o * 128: cfg.oq + (o + 1) * 128], DC,
                  rhs_fn, rb_fn, NH, mk_evac(q_st.t, 32, 128.0 ** -0.5, 1), banks=(0, 1, 2, 3))
        P.dma("sp", qTs[:, t0:t0 + T].rearrange("(c p) t -> p c t", p=128), q_st.t[:, 0:NH, :],
              reads=[act_b[32 + j] for j in range(NH)], writes=[b_q[i]], sem=q_st.sem)
        proj_rows(lambda o, k0, nk: w_in[k0 * 128:(k0 + nk) * 128, cfg.ok + o * 128: cfg.ok + (o + 1) * 128], DC,
                  rhs_fn, rb_fn, NH, mk_evac(k_st.t, 48, None, 1), banks=(0, 1, 2, 3))
        P.dma("sp", kTs[:, t0:t0 + T].rearrange("(c p) t -> p c t", p=128), k_st.t[:, 0:NH, :],
              reads=[act_b[48 + j] for j in range(NH)], writes=[b_k[i]], sem=k_st.sem)

        def ev_u(o, ps):
            P.op("dve", lambda E: E.tensor_copy(u_st.t[:, o, :], ps.t[:]), reads=[ps.b], writes=[act_b[64 + 2 * o], act_b[65 + 2 * o]])
        proj_rows(lambda o, k0, nk: w_in[k0 * 128:(k0 + nk) * 128, cfg.ou + o * 128: cfg.ou + (o + 1) * 128], DC,
                  rhs_fn, rb_fn, 8, ev_u, banks=(0, 1, 2, 3))
        P.dma("sp", uTs[:, t0:t0 + T].rearrange("(c p) t -> p c t", p=128), u_st.t[:, 0:8, :],
              reads=[act_b[64 + j] for j in range(16)], writes=[b_u[i]], sem=u_st.sem)

        nqc = 2 * MH
        o0 = 0
        while o0 < nqc:
            npc = min(6, nqc - o0)

            def ev_qc(o, ps, o0=o0):
                P.op("dve", lambda E: E.tensor_scalar(qc_st.t[:, o, :], ps.t[:], 256.0 ** -0.5, None, ALU.mult),
                     reads=[ps.b], writes=[act_b[80 + o]])
            proj_rows(lambda o, k0, nk, o0=o0: w_in[k0 * 128:(k0 + nk) * 128, cfg.oqc + (o0 + o) * 128: cfg.oqc + (o0 + o + 1) * 128], DC,
                      rhs_fn, rb_fn, npc, ev_qc, banks=(0, 1, 2, 3))
            P.dma("sp", qcTs[o0 * 128:(o0 + npc) * 128, t0:t0 + T].rearrange("(c p) t -> p c t", p=128), qc_st.t[:, 0:npc, :],
                  reads=[act_b[80 + j] for j in range(npc)], writes=[b_qc[i]], sem=qc_st.sem)
            o0 += npc

        KG = 8
        for vb in range(cfg.NAW // 512):
            for kg in range(DC // KG if DC >= KG else 1):
                nk = min(KG, DC)
                s = load_w(w_in[kg * KG * 128:(kg * KG + nk) * 128, cfg.ov + vb * 512: cfg.ov + (vb + 1) * 512], nk, wide=True)
                wv = s.tv
                for kk in range(nk):
                    k = kg * KG + kk
                    for tb in range(4):
                        ps = PS[4 + tb]
                        P.op("pe", (lambda ps, wv, kk, k, tb: lambda E: E.matmul(
                            ps.t[:], lhsT=H2[:, k, tb * 128:(tb + 1) * 128], rhs=wv[:, kk, :],
                            start=(k == 0), stop=(k == DC - 1)))(ps, wv, kk, k, tb),
                             reads=[s.b, act_b[k]], writes=[ps.b], inc=(k == DC - 1))
            for tb in range(4):
                P.op("dve", (lambda tb: lambda E: E.tensor_copy(q_st.t[:, tb, :], PS[4 + tb].t[:]))(tb),
                     reads=[PS[4 + tb].b], writes=[act_b[32 + tb]])
            P.dma("sp", vS[t0:t0 + T, vb * 512:(vb + 1) * 512].rearrange("(tb p) f -> p tb f", p=128), q_st.t[:, 0:4, :],
                  reads=[act_b[32 + j] for j in range(4)], writes=[b_v[i]], sem=q_st.sem)

    ATT = at("ATT", [128, 16, 512], BF16, R_ACT)
    OC = at("OC", [128, 8, 512], BF16, R_ACT + 16384)
    MIX = at("MIX", [128, 8, 512], BF16, R_ACT + 24576)
    MRG = at("MRG", [128, 32, 512], BF16, R_ACT + 32768)
    maskL = Reg(at("maskL", [8, 1024], BF16, R_ACT + 65536), None, P.new_sem("mkl"))
    KmT = at("KmT", [128, 8, 256], BF16, R_ACT + 67584)
    Vm = at("Vm", [128, 2, 1024], BF16, R_ACT + 71680)
    qcT = Reg(at("qcT", [128, 8, 512], BF16, R_ACT + 75776), None, P.new_sem("qcl"))
    wpool = Reg(at("wpool", [128, 8, 256], BF16, R_ACT + 83968), None, P.new_sem("wpl"))
    HM = at("HM", [128, 32, 256], BF16, R_Y)
    Ec = [at("Ec%d" % i, [128, 512], BF16, R_Y + 16384 + i * 1024) for i in range(2)]
    En = at("En", [128, 16, 512], BF16, R_Y)
    QTa = Reg(at("QTa", [128, 16, 512], BF16, R_Y + 16384), None, P.new_sem("qtl"))
    KTa = Reg(at("KTa", [128, 16, 1024], BF16, R_Y + 32768), None, P.new_sem("ktl"))
    Va = Reg(at("Va", [128, 8, 2048], BF16, R_W), None, P.new_sem("val"))
    bias_sl = [Reg(at("bsl%d" % i, [128, 1408], BF16, XS_BASE + (2 + 2 * i) * 2048), None, P.new_sem("bsl%d" % i)) for i in range(2)]
    bias_bufs = [[XS[2 + 2 * i].b, XS[3 + 2 * i].b] for i in range(2)]
    U_t = Reg(at("U_t", [128, 2, 528], F32, R_Y), None, P.new_sem("utl"))
    UM = at("UM", [128, 2, 528], F32, R_Y + 4608)
    PA = at("PA", [128, 2, 528], F32, R_Y + 9216)
    PB = at("PB", [128, 2, 528], F32, R_Y + 13824)
    PV = Reg(at("PVt", [128, 528], F32, R_Y + 18432), None, P.new_sem("pvl"))
    PI = Reg(at("PIt", [128, 4, 512], F32, R_Y + 20992), None, P.new_sem("pil"))
    PO = at("PO", [128, 8, 512], BF16, R_Y + 29696)

    def yb(lo, hi):
        return [y_b[i] for i in range(lo, hi)]

    def ab(lo, hi):
        return [act_b[i] for i in range(lo, hi)]

    def ws_all():
        return [w.b for w in wslots]

    def phaseBC(j):
        wctx.update(ph="B", pass0=(j == 0), key=0)
        te = (8 * j + 4) * 64
        tk = 8 * j * 64
        et = [b for b in (j, j + 1)]
        prenorm_from_dram(memT[j], [], 0, G_MEM, HM, lambda c: yb(c // 4, c // 4 + 1), n=256)

        def ev_km(o, ps):
            P.op("dve", lambda E: E.tensor_copy(KmT[:, o, :], ps.t[:, 0:256]), reads=[ps.b], writes=ab(66, 70))
        proj_rows(lambda o, k0, nk: w_kv[k0 * 128:(k0 + nk) * 128, o * 128:(o + 1) * 128], DC,
                  lambda k: HM[:, k, :], lambda k: yb(k // 4, k // 4 + 1), 2 * MH, ev_km, banks=(0, 1))
        KG = 8
        nvb = cfg.MW // 512
        for vb in range(nvb):
            for kg in range(max(1, DC // KG)):
                nk = min(KG, DC)
                s = load_w(w_kv[kg * KG * 128:(kg * KG + nk) * 128, cfg.MW + vb * 512: cfg.MW + (vb + 1) * 512], nk, wide=True)
                wv = s.tv
                for kk in range(nk):
                    k = kg * KG + kk
                    for mb in range(2):
                        ps = PS[2 + mb]
                        P.op("pe", (lambda ps, wv, kk, k, mb: lambda E: E.matmul(
                            ps.t[:], lhsT=HM[:, k, mb * 128:(mb + 1) * 128], rhs=wv[:, kk, :],
                            start=(k == 0), stop=(k == DC - 1)))(ps, wv, kk, k, mb),
                             reads=[s.b] + yb(k // 4, k // 4 + 1), writes=[ps.b], inc=(k == DC - 1))
            for mb in range(2):
                P.op("dve", (lambda mb, vb: lambda E: E.tensor_copy(Vm[:, mb, vb * 512:(vb + 1) * 512], PS[2 + mb].t[:]))(mb, vb),
                     reads=[PS[2 + mb].b], writes=ab(70, 74))
        P.dma("sp", qcT.t[:, 0:2 * MH, :], qcTs[:, te:te + T].rearrange("(c p) t -> p c t", p=128),
              reads=[b_qc[e] for e in et], writes=ab(74, 82), sem=qcT.sem)
        for hc in range(MH):
            for mb in range(2):
                ps = PS[mb]
                for dc in range(2):
                    P.op("pe", (lambda ps, hc, mb, dc: lambda E: E.matmul(
                        ps.t[:], lhsT=KmT[:, hc * 2 + dc, mb * 128:(mb + 1) * 128], rhs=qcT.t[:, hc * 2 + dc, :],
                        start=(dc == 0), stop=(dc == 1)))(ps, hc, mb, dc),
                         reads=ab(66, 70) + ab(74, 82), writes=[ps.b], inc=(dc == 1))
                P.op("act", (lambda ps, mb: lambda E: E.activation(Ec[mb][:], ps.t[:], AF.Exp))(ps, mb), reads=[ps.b], writes=[y_b[8]])
            for dc in range(2):
                ps = PS[2 + dc]
                for mb in range(2):
                    P.op("pe", (lambda ps, hc, mb, dc: lambda E: E.matmul(
                        ps.t[:], lhsT=Vm[:, mb, (hc * 2 + dc) * 128:(hc * 2 + dc + 1) * 128], rhs=Ec[mb][:],
                        start=(mb == 0), stop=(mb == 1)))(ps, hc, mb, dc),
                         reads=ab(70, 74) + [y_b[8]], writes=[ps.b], inc=(mb == 1))
            psd = PS[4]
            for mb in range(2):
                P.op("pe", (lambda mb: lambda E: E.matmul(psd.t[:], lhsT=ones_b.t[:], rhs=Ec[mb][:], start=(mb == 0), stop=(mb == 1)))(mb),
                     reads=[ones_b.b, y_b[8]], writes=[psd.b], inc=(mb == 1))
            ft = next_ft()
            P.op("dve", (lambda ft: lambda E: E.reciprocal(ft.t[:], psd.t[:]))(ft), reads=[psd.b], writes=[ft.b])
            for dc in range(2):
                P.op("dve", (lambda ft, hc, dc: lambda E: E.tensor_tensor(OC[:, hc * 2 + dc, :], PS[2 + dc].t[:], ft.t[:], ALU.mult))(ft, hc, dc),
                     reads=[PS[2 + dc].b, ft.b], writes=ab(16, 24))
        P.dma("pool", wpool.t[:], w_pool.rearrange("(g p) f -> p g f", p=128), [], ab(82, 86), wpool.sem)
        P.dma("sp", PV.t[:], pvalid_d[j], [], yb(9, 11), PV.sem)
        P.dma("sp", PI.t[:], pinv_d[j].rearrange("p (g t) -> p g t", t=512), [], yb(10, 15), PI.sem)
        for g in range(4):
            P.dma("sp", U_t.t[:], uTs[g * 256:(g + 1) * 256, te - 8:te + 520].rearrange("(c p) t -> p c t", p=128),
                  reads=[b_u[e] for e in et], writes=yb(0, 3), sem=U_t.sem)
            for c in range(2):
                P.op("dve", (lambda c: lambda E: E.tensor_tensor(UM[:, c, :], U_t.t[:, c, :], PV.t[:], ALU.mult))(c),
                     reads=yb(0, 3) + yb(9, 11), writes=yb(2, 5))
            src, dst = UM, PA
            P.op("dve", lambda E: E.tensor_tensor(PA[:, :, 1:528], UM[:, :, 0:527], UM[:, :, 1:528], ALU.add), reads=yb(2, 5), writes=yb(4, 7))
            cur_t, cur_b, oth_t, oth_b = PA, yb(4, 7), PB, yb(6, 9)
            lo, hi = 1, 528
            for lvl in range(g):
                sh = 1 << lvl
                nlo, nhi = lo + sh, hi - sh
                P.op("dve", (lambda cur_t, oth_t, nlo, nhi, sh: lambda E: E.tensor_tensor(
                    oth_t[:, :, nlo:nhi], cur_t[:, :, nlo - sh:nhi - sh], cur_t[:, :, nlo + sh:nhi + sh], ALU.add))(cur_t, oth_t, nlo, nhi, sh),
                     reads=cur_b, writes=oth_b)
                cur_t, cur_b, oth_t, oth_b = oth_t, oth_b, cur_t, cur_b
                lo, hi = nlo, nhi
            for c in range(2):
                P.op("dve", (lambda cur_t, c, g: lambda E: E.tensor_tensor(cur_t[:, c, 8:520], cur_t[:, c, 8:520], PI.t[:, g, :], ALU.mult))(cur_t, c, g),
                     reads=cur_b + yb(10, 15), writes=cur_b)
                P.op("dve", (lambda cur_t, c, g: lambda E: E.tensor_tensor(PO[:, g * 2 + c, :], cur_t[:, c, 8:520], U_t.t[:, c, 8:520], ALU.subtract))(cur_t, c, g),
                     reads=cur_b + yb(0, 3), writes=yb(14, 19))
            for oc in range(2):
                ps = PS[oc]
                for ic in range(2):
                    P.op("pe", (lambda ps, g, ic, oc: lambda E: E.matmul(
                        ps.t[:], lhsT=wpool.t[:, g * 2 + ic, oc * 128:(oc + 1) * 128], rhs=PO[:, g * 2 + ic, :],
                        start=(ic == 0), stop=(ic == 1)))(ps, g, ic, oc),
                         reads=ab(82, 86) + yb(14, 19), writes=[ps.b], inc=(ic == 1))
                P.op("dve", (lambda ps, g, oc: lambda E: E.tensor_scalar(MIX[:, g * 2 + oc, :], ps.t[:], pscale.t[:, g * 2 + oc: g * 2 + oc + 1], None, ALU.mult))(ps, g, oc),
                     reads=[ps.b, pscale.b], writes=ab(24, 32))
        P.dma("sp", QTa.t[:, 0:NH, :], qTs[:, te:te + T].rearrange("(c p) t -> p c t", p=128),
              reads=[b_q[e] for e in et], writes=yb(8, 16), sem=QTa.sem)
        P.dma("sp", KTa.t[:, 0:NH, :], kTs[:, tk:tk + 1024].rearrange("(c p) t -> p c t", p=128),
              reads=[b_k[e] for e in et], writes=yb(16, 32), sem=KTa.sem)
        P.dma("sp", Va.t[:, :, 0:cfg.NAW], vS[tk:tk + 1024, :].rearrange("(b p) f -> p b f", p=128),
              reads=[b_v[e] for e in et], writes=[wslots[i].b for i in range(4)], sem=Va.sem)
        P.dma("pool", maskL.t[:], maskL_d[j], [], ab(64, 66), maskL.sem)
        for h in range(NH):
            bs = bias_sl[h % 2]
            P.dma("pool", bs.t[:], biasT_d[h], [], bias_bufs[h % 2], bs.sem)
            eb = (h % 2) * 8
            for bp in range(8):
                ps = PS[bp % 4]
                j0 = (14 - 2 * bp) * 64
                P.op("pe", (lambda ps, h, bp: lambda E: E.matmul(ps.t[:], lhsT=KTa.t[:, h, bp * 128:(bp + 1) * 128], rhs=QTa.t[:, h, :],
                                                                start=True, stop=False))(ps, h, bp),
                     reads=yb(8, 32), writes=[ps.b], inc=False)
                P.op("pe", (lambda ps, bs, j0: lambda E: E.matmul(ps.t[:], lhsT=ident_b.t[:], rhs=bs.t[:, j0:j0 + 512],
                                                                 start=False, stop=False))(ps, bs, j0),
                     reads=[ident_b.b] + bias_bufs[h % 2], writes=[ps.b], inc=False)
                P.op("pe", (lambda ps, bp: lambda E: E.matmul(ps.t[:], lhsT=maskL.t[:, bp * 128:(bp + 1) * 128], rhs=SEL.t[:],
                                                             start=False, stop=True))(ps, bp),
                     reads=ab(64, 66) + [SEL.b], writes=[ps.b], inc=True)
                P.op("act", (lambda ps, eb, bp: lambda E: E.activation(En[:, eb + bp, :], ps.t[:], AF.Exp))(ps, eb, bp),
                     reads=[ps.b], writes=yb(eb // 2 + bp // 2, eb // 2 + bp // 2 + 1))
            pso, psd = PS[4 + h % 2], PS[6 + h % 2]
            for bp in range(8):
                P.op("pe", (lambda pso, h, bp, eb: lambda E: E.matmul(pso.t[:], lhsT=Va.t[:, bp, h * 128:(h + 1) * 128], rhs=En[:, eb + bp, :],
                                                                     start=(bp == 0), stop=(bp == 7)))(pso, h, bp, eb),
                     reads=[wslots[i].b for i in range(4)] + yb(eb // 2, eb // 2 + 4), writes=[pso.b], inc=(bp == 7))
            for bp in range(8):
                P.op("pe", (lambda psd, bp, eb: lambda E: E.matmul(psd.t[:], lhsT=ones_b.t[:], rhs=En[:, eb + bp, :],
                                                                  start=(bp == 0), stop=(bp == 7)))(psd, bp, eb),
                     reads=[ones_b.b] + yb(eb // 2, eb // 2 + 4), writes=[psd.b], inc=(bp == 7))
            ft = next_ft()
            P.op("dve", (lambda ft, psd: lambda E: E.reciprocal(ft.t[:], psd.t[:]))(ft, psd), reads=[psd.b], writes=[ft.b])
            P.op("dve", (lambda ft, pso, h: lambda E: E.tensor_tensor(ATT[:, h, :], pso.t[:], ft.t[:], ALU.mult))(ft, pso, h),
                 reads=[pso.b, ft.b], writes=ab(0, 16))
        P.dma("sp", Hb[:, 0:DC, :], h2Ts[:, te:te + T].rearrange("(c p) t -> p c t", p=128),
              reads=[b_h2[e] for e in et], writes=[y_b[c] for c in range((DC + 1) // 2)], sem=sem_hb)
        for m in range(DC):
            specs = [
                (w_in, cfg.og + 0 * D + m * 128, DC, lambda k: Hb[:, k, :], hb_bufs, None),
                (w_ao, m * 128, NH, lambda k: ATT[:, k, :], lambda k: ab(0, 16), 0),
                (w_in, cfg.og + 1 * D + m * 128, DC, lambda k: Hb[:, k, :], hb_bufs, None),
                (w_bo, m * 128, 8, lambda k: MIX[:, k, :], lambda k: ab(24, 32), 1),
                (w_in, cfg.og + 2 * D + m * 128, DC, lambda k: Hb[:, k, :], hb_bufs, None),
                (w_co, m * 128, 2 * MH, lambda k: OC[:, k, :], lambda k: ab(16, 24), 2),
            ]
            for br in range(3):
                psg, psy = PS[2 * br], PS[2 * br + 1]
                for (ps, spec) in ((psg, specs[2 * br]), (psy, specs[2 * br + 1])):
                    wsrc, col, nkc, rfn, rbf, _ = spec
                    s = load_w(wsrc[0:nkc * 128, col:col + 128], nkc)
                    for k in range(nkc):
                        P.op("pe", (lambda ps, s, k, rfn, nkc: lambda E: E.matmul(ps.t[:], lhsT=s.t[:, k, :], rhs=rfn(k),
                                                                              start=(k == 0), stop=(k == nkc - 1)))(ps, s, k, rfn, nkc),
                             reads=[s.b] + rbf(k), writes=[ps.b], inc=(k == nkc - 1))
                ft = next_ft()
                P.op("act", (lambda ft, psg, br, m: lambda E: E.activation(ft.t[:], psg.t[:], AF.Sigmoid,
                                                                         bias=bgate.t[:, br * DC + m: br * DC + m + 1]))(ft, psg, br, m),
                     reads=[psg.b, bgate.b], writes=[ft.b])
                if br == 0:
                    acc = next_ft()
                    P.op("dve", (lambda acc, ft, psy: lambda E: E.tensor_tensor(acc.t[:], ft.t[:], psy.t[:], ALU.mult))(acc, ft, psy),
                         reads=[ft.b, psy.b], writes=[acc.b])
                else:
                    P.op("dve", (lambda ft, psy: lambda E: E.tensor_tensor(ft.t[:], ft.t[:], psy.t[:], ALU.mult))(ft, psy),
                         reads=[ft.b, psy.b], writes=[ft.b])
                    if br == 1:
                        P.op("dve", (lambda acc, ft: lambda E: E.tensor_tensor(acc.t[:], acc.t[:], ft.t[:], ALU.add))(acc, ft),
                             reads=[ft.b, acc.b], writes=[acc.b])
                    else:
                        P.op("dve", (lambda acc, ft, m: lambda E: E.tensor_tensor(MRG[:, m, :], acc.t[:], ft.t[:], ALU.add))(acc, ft, m),
                             reads=[ft.b, acc.b], writes=ab(32 + m, 33 + m))
        proj_rows(lambda o, k0, nk: w_o[k0 * 128:(k0 + nk) * 128, o * 128:(o + 1) * 128], DC,
                  lambda k: MRG[:, k, :], lambda k: ab(32 + k, 33 + k), DC, evac_to_Y_with_stats, banks=(4, 5))
        residual_finish(x1T, [b_x1[e] for e in et], te, lambda c: gcol(G_MIXPOST, c))
        store_Y(x2T, [b_x2[j]], j * T)
        ffn_stage(x2T, [b_x2[j]], j * T, w2g, w2u, w2d, G_F2PRE, 1)
        stats_finish()
        for c in range(DC):
            P.op("dve", (lambda c: lambda E: E.scalar_tensor_tensor(Yf[:, c, :], Yf[:, c, :], gcol(G_FINAL, c), RSTD.t[:], ALU.mult, ALU.mult))(c),
                 reads=[y_b[c], RSTD.b, gains.b], writes=[y_b[c]])
        store_Y(yT, [b_y[j]], j * T)

    for f in range(FC):
        conv_list.append((keyF2 + 2 * f, w2g[:, f * 128:(f + 1) * 128], DC))
        conv_list.append((keyF2 + 2 * f + 1, w2u[:, f * 128:(f + 1) * 128], DC))
    for i in range(NEXT):
        phaseA(i)
    for j in range(own):
        phaseBC(j)
    P.final_wait("sp", [sem_stY])

    with nc.Block() as block:
        block.sync(P.replay("sp"))
        block.gpsimd(P.replay("pool"))
        block.tensor(P.replay("pe"))
        block.scalar(P.replay("act"))
        block.vector(P.replay("dve"))
    return nc


def _tile_table(cfg):
    tab = []
    for s, n in enumerate(cfg.seq_tiles):
        for i in range(n):
            tab.append((s, i, n))
    return tab


def prepare_inputs(cfg, inp):
    DC, D, T, own, NEXT, NH = cfg.DC, cfg.D, cfg.T, cfg.own, cfg.next, cfg.NH
    f32 = np.float32
    xs = [np.asarray(inp["x_prompt"], f32)[b] for b in range(inp["x_prompt"].shape[0])] + \
         [np.asarray(inp["x_sample"], f32)[b] for b in range(inp["x_sample"].shape[0])]
    mems = [np.asarray(inp["mem_prompt"], f32)[b] for b in range(inp["mem_prompt"].shape[0])] + \
           [np.asarray(inp["mem_sample"], f32)[b] for b in range(inp["mem_sample"].shape[0])]
    assert len(xs) == len(cfg.seq_tiles)
    tab = _tile_table(cfg)
    G = len(tab)
    xall = np.concatenate(xs, axis=0)
    xallT = np.zeros((D, (G + 2) * T), f32)
    xallT[:, T:T + G * T] = xall.T
    memsT = [np.ascontiguousarray(m.T) for m in mems]

    def gvec(v):
        return np.asarray(v, f32).reshape(DC, 128).T

    gains = np.concatenate([gvec(inp[k][0]) for k in
                            ("g_ffn1_pre", "g_ffn1_post", "g_mix_pre", "g_mem", "g_mix_post", "g_ffn2_pre", "g_ffn2_post", "g_final")], axis=1)
    bgate = np.concatenate([gvec(inp["b_gate"][0][i]) for i in range(3)], axis=1)
    pscale = np.asarray(inp["pool_scale"][0], f32).reshape(8, 128).T
    rpb = np.asarray(inp["rpb"][0], f32)
    rpb_ext = np.concatenate([rpb.reshape(NH, -1), np.full((NH, 1), NEG, f32)], axis=1)
    NEGI = 15 * 31
    cols = np.arange(64)
    col_start = np.clip(cols - 8, 0, 48)
    idx = np.full((128, 22, 64), NEGI, np.int64)
    for bprime in range(2):
        for kc in range(64):
            p = bprime * 64 + kc
            for jj in range(22):
                ro = (14 - jj) + bprime + 3
                if ro < 0 or ro > 14:
                    continue
                ok = (kc >= col_start) & (kc < col_start + 16)
                co = kc - cols + 15
                idx[p, jj, ok] = ro * 31 + co[ok]
    biasT = np.ascontiguousarray(rpb_ext[:, idx.reshape(-1)].reshape(NH, 128, 22 * 64))
    sel = np.zeros((8, 512), f32)
    for a in range(8):
        sel[a, a * 64:(a + 1) * 64] = 1.0
    ident = np.eye(128, dtype=f32)
    w_pool = np.asarray(inp["w_pool"][0], f32).reshape(4 * 256, 256)

    shared = {
        "w1_gate": np.asarray(inp["w1_gate"][0], f32), "w1_up": np.asarray(inp["w1_up"][0], f32), "w1_down": np.asarray(inp["w1_down"][0], f32),
        "w2_gate": np.asarray(inp["w2_gate"][0], f32), "w2_up": np.asarray(inp["w2_up"][0], f32), "w2_down": np.asarray(inp["w2_down"][0], f32),
        "w_in": np.asarray(inp["w_in"][0], f32), "w_mem_kv": np.asarray(inp["w_mem_kv"][0], f32),
        "w_a_out": np.asarray(inp["w_a_out"][0], f32), "w_b_out": np.asarray(inp["w_b_out"][0], f32), "w_c_out": np.asarray(inp["w_c_out"][0], f32),
        "w_o": np.asarray(inp["w_o"][0], f32), "w_pool": w_pool,
        "gains": np.ascontiguousarray(gains), "bgate": np.ascontiguousarray(bgate), "pscale": np.ascontiguousarray(pscale),
        "biasT": biasT, "sel": sel, "ident": ident,
    }
    in_maps = []
    for c in range(cfg.n_cores):
        g0 = c * own
        tstart = T + g0 * T - 256
        xT = np.ascontiguousarray(xallT[:, tstart:tstart + NEXT * T])
        memT = np.stack([memsT[tab[g0 + j][0]] for j in range(own)], axis=0)
        maskL = np.full((own, 8, 8, 2, 64), NEG, f32)
        pvalid = np.zeros((own, 128, 528), f32)
        pinv = np.zeros((own, 128, 4, 512), f32)
        for j in range(own):
            s, i, n = tab[g0 + j]
            first, last = (i == 0), (i == n - 1)
            for a in range(8):
                rs = a - 4
                if first:
                    rs = max(rs, 0)
                if last:
                    rs = min(rs, 0)
                for b in range(rs + 4, rs + 12):
                    maskL[j, a, b // 2, b % 2, :] = 0.0
            Ls, Le = -i * T, (n - i) * T
            ii = np.arange(528) - 8
            pvalid[j, :, :] = ((ii >= Ls) & (ii < Le)).astype(f32)[None, :]
            t = np.arange(512)
            for g, w in enumerate((2, 4, 8, 16)):
                lo = np.maximum(t - w // 2, Ls)
                hi = np.minimum(t + w // 2, Le)
                pinv[j, :, g, :] = (1.0 / (hi - lo).astype(f32))[None, :]
        m = dict(shared)
        m.update({"xT": xT, "memT": np.ascontiguousarray(memT), "maskL": np.ascontiguousarray(maskL.reshape(own, 8, 1024)),
                  "pvalid": pvalid, "pinv": np.ascontiguousarray(pinv.reshape(own, 128, 2048))})
        in_maps.append(m)
    return in_maps


def assemble_outputs(cfg, results, inp):
    T, own = cfg.T, cfg.own
    yall = np.concatenate([np.asarray(r["yT"]) for r in results], axis=1)
    y = np.ascontiguousarray(yall.T)
    outs = []
    off = 0
    seq_i = 0
    for key in ("x_prompt", "x_sample"):
        B = inp[key].shape[0]
        L = inp[key].shape[1]
        parts = []
        for b in range(B):
            parts.append(y[off:off + L])
            off += L
            seq_i += 1
        outs.append(np.stack(parts, axis=0).astype(np.float32))
    return tuple(outs)


_CACHE = {}


def run(cfg, inp):
    in_maps = prepare_inputs(cfg, inp)
    key = (cfg.DC, cfg.FC, cfg.NH, cfg.MH, cfg.seq_tiles)
    nc = build_program(cfg)
    res = run_bass_kernel_spmd(nc, in_maps, core_ids=list(range(cfg.n_cores)))
    return assemble_outputs(cfg, res.results, inp)


def kernel(**inputs):
    cfg = Cfg()
    return run(cfg, inputs)
```
